# Optimizing a Trainium2 kernel written in Bass

```python
import math
import jax, jax.numpy as jnp
from jax import lax
import numpy as np

D_MODEL = 2048
BATCH = 8
SEQ = 2048
DEPTH = 1

N_HEADS = 8
HEAD_DIM = 128
V_HEAD_DIM = 2 * HEAD_DIM
QK_WIDTH = N_HEADS * 2 * HEAD_DIM
ATTN_WIDTH = N_HEADS * V_HEAD_DIM
ROPE_THETA = 10000.0
Q_BLOCK = 128
SUBLN_EPS = 1e-5
LAMBDA_STD = 0.1
CONV_CH = D_MODEL
CONV_WIDTH = 31
LN_EPS = 1e-5
D_FF = 4 * D_MODEL
NORM_EPS = 1e-6
NEG_INF = -1e30
IN_WIDTH = 2 * QK_WIDTH + ATTN_WIDTH + 2 * CONV_CH + 2 * D_MODEL
SPLITS = (QK_WIDTH, 2 * QK_WIDTH, 2 * QK_WIDTH + ATTN_WIDTH,
          2 * QK_WIDTH + ATTN_WIDTH + 2 * CONV_CH,
          2 * QK_WIDTH + ATTN_WIDTH + 2 * CONV_CH + D_MODEL)

kernel_name = 'hybrid_diffattn_conformer_gated'


def rms_norm(x, gain, eps):
    x32 = x.astype(jnp.float32)
    y = x32 * lax.rsqrt(jnp.mean(x32 * x32, axis=-1, keepdims=True) + eps)
    return (y * gain.astype(jnp.float32)).astype(x.dtype)


def layer_norm(x, gain, bias, eps):
    x32 = x.astype(jnp.float32)
    mu = jnp.mean(x32, axis=-1, keepdims=True)
    xc = x32 - mu
    y = xc * lax.rsqrt(jnp.mean(xc * xc, axis=-1, keepdims=True) + eps)
    return (y * gain.astype(jnp.float32) + bias.astype(jnp.float32)).astype(x.dtype)


def rope(x, pos):
    d = x.shape[-1]
    inv_freq = 1.0 / (ROPE_THETA ** (jnp.arange(0, d, 2, dtype=jnp.float32) / d))
    ang = pos[:, None] * inv_freq[None, :]
    cos = jnp.cos(ang)[None, :, None, None, :].astype(x.dtype)
    sin = jnp.sin(ang)[None, :, None, None, :].astype(x.dtype)
    x1, x2 = jnp.split(x, 2, axis=-1)
    return jnp.concatenate([x1 * cos - x2 * sin, x2 * cos + x1 * sin], axis=-1)


def diff_attention(q, k, v, lam):
    S = q.shape[1]
    outs = []
    for i in range(S // Q_BLOCK):
        end = (i + 1) * Q_BLOCK
        qb = q[:, i * Q_BLOCK:end]
        kb = k[:, :end]
        vb = v[:, :end]
        s = jnp.einsum('bqhmd,bkhmd->bhmqk', qb, kb).astype(jnp.float32)
        q_pos = i * Q_BLOCK + jnp.arange(Q_BLOCK)
        mask = jnp.arange(end)[None, :] <= q_pos[:, None]
        p = jax.nn.softmax(jnp.where(mask, s, NEG_INF), axis=-1)
        a = p[:, :, 0] - lam * p[:, :, 1]
        outs.append(jnp.einsum('bhqk,bkhe->bqhe', a.astype(vb.dtype), vb))
    return jnp.concatenate(outs, axis=1)


def causal_depthwise_conv(u, w, b):
    C = u.shape[-1]
    y = lax.conv_general_dilated(
        u, w[:, None, :].astype(u.dtype), window_strides=(1,),
        padding=((CONV_WIDTH - 1, 0),), dimension_numbers=('NWC', 'WIO', 'NWC'),
        feature_group_count=C)
    return y + b.astype(u.dtype)


def setup_inputs(seed: int = 0) -> dict:
    key = jax.random.key(seed)
    ks = jax.random.split(key, 24)
    f32 = jnp.float32
    nrm = lambda k, shape, s: jax.random.normal(k, shape, f32) * s
    gain = lambda k, n: 1.0 + nrm(k, (DEPTH, n), 0.02)
    return {
        'x': nrm(ks[0], (BATCH, SEQ, D_MODEL), 1.0),
        'pre_mix_gain': gain(ks[1], D_MODEL),
        'w_in': nrm(ks[2], (DEPTH, D_MODEL, IN_WIDTH), D_MODEL ** -0.5),
        'lambda_q1': nrm(ks[3], (DEPTH, HEAD_DIM), LAMBDA_STD),
        'lambda_k1': nrm(ks[4], (DEPTH, HEAD_DIM), LAMBDA_STD),
        'lambda_q2': nrm(ks[5], (DEPTH, HEAD_DIM), LAMBDA_STD),
        'lambda_k2': nrm(ks[6], (DEPTH, HEAD_DIM), LAMBDA_STD),
        'subln_gain': gain(ks[7], V_HEAD_DIM),
        'glu_bias': nrm(ks[8], (DEPTH, 2 * CONV_CH), 0.02),
        'dw_kernel': nrm(ks[9], (DEPTH, CONV_WIDTH, CONV_CH), CONV_WIDTH ** -0.5),
        'dw_bias': nrm(ks[10], (DEPTH, CONV_CH), 0.02),
        'conv_ln_gain': gain(ks[11], CONV_CH),
        'conv_ln_bias': nrm(ks[12], (DEPTH, CONV_CH), 0.02),
        'w_conv_out': nrm(ks[13], (DEPTH, CONV_CH, D_MODEL), CONV_CH ** -0.5),
        'b_conv_out': nrm(ks[14], (DEPTH, D_MODEL), 0.02),
        'w_out': nrm(ks[15], (DEPTH, D_MODEL, D_MODEL), D_MODEL ** -0.5),
        'post_mix_gain': gain(ks[16], D_MODEL),
        'pre_ff_gain': gain(ks[17], D_MODEL),
        'w_ff1': nrm(ks[18], (DEPTH, D_MODEL, D_FF), D_MODEL ** -0.5),
        'w_ff2': nrm(ks[19], (DEPTH, D_FF, D_MODEL), D_FF ** -0.5),
        'post_ff_gain': gain(ks[20], D_MODEL),
    }


def reference(x, pre_mix_gain, w_in, lambda_q1, lambda_k1, lambda_q2, lambda_k2, subln_gain,
              glu_bias, dw_kernel, dw_bias, conv_ln_gain, conv_ln_bias, w_conv_out, b_conv_out,
              w_out, post_mix_gain, pre_ff_gain, w_ff1, w_ff2, post_ff_gain):
    B, S, _ = x.shape
    pos = jnp.arange(S, dtype=jnp.float32)
    h = x
    for l in range(DEPTH):
        lam_init = 0.8 - 0.6 * math.exp(-0.3 * l)
        u = rms_norm(h, pre_mix_gain[l], NORM_EPS)
        z = u @ w_in[l]
        zq, zk, zv, zc, zga, zgc = jnp.split(z, SPLITS, axis=-1)
        q = rope(zq.reshape(B, S, N_HEADS, 2, HEAD_DIM), pos) * (HEAD_DIM ** -0.5)
        k = rope(zk.reshape(B, S, N_HEADS, 2, HEAD_DIM), pos)
        v = zv.reshape(B, S, N_HEADS, V_HEAD_DIM)
        f32 = jnp.float32
        lam = (jnp.exp(jnp.sum(lambda_q1[l].astype(f32) * lambda_k1[l].astype(f32)))
               - jnp.exp(jnp.sum(lambda_q2[l].astype(f32) * lambda_k2[l].astype(f32)))
               + lam_init)
        o = diff_attention(q, k, v, lam)
        o = rms_norm(o, subln_gain[l], SUBLN_EPS) * (1.0 - lam_init)
        attn_out = o.reshape(B, S, ATTN_WIDTH)
        zc = zc + glu_bias[l]
        c_val, c_gate = jnp.split(zc, 2, axis=-1)
        c = c_val * jax.nn.sigmoid(c_gate)
        c = causal_depthwise_conv(c, dw_kernel[l], dw_bias[l])
        c = jax.nn.silu(layer_norm(c, conv_ln_gain[l], conv_ln_bias[l], LN_EPS))
        conv_out = c @ w_conv_out[l] + b_conv_out[l]
        m = jax.nn.sigmoid(zga) * attn_out + jax.nn.sigmoid(zgc) * conv_out
        h = h + rms_norm(m @ w_out[l], post_mix_gain[l], NORM_EPS)
        u = rms_norm(h, pre_ff_gain[l], NORM_EPS)
        f = jnp.square(jax.nn.relu(u @ w_ff1[l])) @ w_ff2[l]
        h = h + rms_norm(f, post_ff_gain[l], NORM_EPS)
    return h
```

```python
import contextlib
import math
import numpy as np
import concourse.bass as bass
import concourse.mybir as mybir
from concourse.bass_utils import run_bass_kernel_spmd

F32 = mybir.dt.float32
BF16 = mybir.dt.bfloat16
AF = mybir.ActivationFunctionType
ALU = mybir.AluOpType
AX = mybir.AxisListType

D = 2048
SEQ = 2048
NH = 8
KC = 16
NT = 16
TB = 512
NTB = SEQ // TB
DFF = 8192
C_Q, C_K, C_V, C_CV, C_CG, C_GA, C_GC = 0, 2048, 4096, 6144, 8192, 10240, 12288
IN_W = 14336
LAM_INIT = 0.8 - 0.6 * math.exp(-0.3 * 0)
CONV_W = 31
HALO = CONV_W - 1
COL_BV, COL_BG, COL_DWB, COL_LNG, COL_LNB, COL_BCO, COL_SUB = 0, 16, 32, 48, 64, 80, 96
NCOL = 98
DVE_TAPS = 23


class Buf:
    __slots__ = ("w", "r", "sem", "cnt", "excl")

    def __init__(self):
        self.excl = False
        self.w = None
        self.r = {}
        self.sem = None
        self.cnt = 0


class Sched:
    def __init__(self, nc, stack):
        self.nc = nc
        self.stack = stack
        self.eng = {"pe": nc.tensor, "dve": nc.vector, "act": nc.scalar, "pool": nc.gpsimd, "sp": nc.sync}
        self.sem = {}
        self.cnt = {}
        self.seen = {k: {} for k in self.eng}
        self.nsem = 0
        self.dbufs = []
        for k in self.eng:
            self._new_sem(k)

    def _new_sem(self, k):
        self.nsem += 1
        self.sem[k] = self.stack.enter_context(self.nc.semaphore(f"s{self.nsem}_{k}"))
        self.cnt[k] = 0

    def rotate(self):
        for k in self.eng:
            if self.cnt[k] > 6000:
                self._new_sem(k)

    def dmabuf(self):
        b = Buf()
        self.nsem += 1
        b.sem = self.stack.enter_context(self.nc.semaphore(f"d{self.nsem}"))
        self.dbufs.append(b)
        return b

    def _wait(self, k, toks):
        seen = self.seen[k]
        for sem, val in toks:
            if seen.get(sem.num, 0) >= val:
                continue
            self.eng[k].wait_ge(sem, val)
            seen[sem.num] = val

    def _deps(self, reads, writes):
        toks = []
        for b in reads:
            if b.w is not None:
                toks.append(b.w)
            if b.excl:
                toks.extend(b.r.values())
        for b in writes:
            if b.w is not None:
                toks.append(b.w)
            toks.extend(b.r.values())
        return toks

    def _commit(self, tok, reads, writes):
        for b in reads:
            old = b.r.get(tok[0].num)
            if old is None or old[1] < tok[1]:
                b.r[tok[0].num] = tok
        for b in writes:
            b.w = tok
            b.r = {}

    def op(self, k, fn, reads=(), writes=()):
        self._wait(k, self._deps(reads, writes))
        ins = fn(self.eng[k])
        self.cnt[k] += 1
        ins.then_inc(self.sem[k], 1)
        tok = (self.sem[k], self.cnt[k])
        self._commit(tok, reads, writes)
        return tok

    def dma(self, k, out, in_, holder, reads=(), writes=()):
        self._wait(k, self._deps(reads, writes))
        ins = self.eng[k].dma_start(out=out, in_=in_)
        holder.cnt += 16
        ins.then_inc(holder.sem, 16)
        tok = (holder.sem, holder.cnt)
        self._commit(tok, reads, writes)
        return tok


def build_nc(debug=False, stop=None):
    nc = bass.Bass("TRN2", target_bir_lowering=False)
    stack = contextlib.ExitStack()
    sc = Sched(nc, stack)

    def din(name, shape, dt=F32):
        return nc.dram_tensor(name, list(shape), dt, kind="ExternalInput").ap()

    x_d = din("x", [SEQ, D])
    w_in_d = din("w_in", [D, IN_W])
    w_co_d = din("w_conv_out", [D, D])
    w_out_d = din("w_out", [D, D])
    w_ff1_d = din("w_ff1", [D, DFF])
    w_ff2_d = din("w_ff2", [DFF, D])
    g_pre_d = din("pre_mix_gain", [1, D])
    g_pm_d = din("post_mix_gain", [1, D])
    g_pf_d = din("pre_ff_gain", [1, D])
    g_po_d = din("post_ff_gain", [1, D])
    cols_d = din("cols", [128, NCOL])
    dwk_d = din("dwk", [128, KC * CONV_W])
    lam_d = din("lamv", [4, 128])
    ident_d = din("ident", [128, 128])
    rot_d = din("rotm", [128, 128])
    mask_d = din("maskT", [128, 2, 128])
    cos_d = din("cosT", [128, SEQ])
    sin_d = din("sinT", [128, SEQ])
    out_d = nc.dram_tensor("out", [SEQ, D], F32, kind="ExternalOutput").ap()
    skind = "ExternalOutput" if debug else "Internal"
    mA_d = nc.dram_tensor("mA_s", [D, SEQ], BF16, kind=skind).ap()
    h1_d = nc.dram_tensor("h1_s", [SEQ, D], F32, kind=skind).ap()
    u2T_d = nc.dram_tensor("u2T_s", [D, SEQ], BF16, kind=skind).ap()
    uT_d = nc.dram_tensor("uT_s", [D, SEQ], BF16, kind=skind).ap()
    uT_v = uT_d.rearrange("(c p) t -> p c t", p=128)
    uT_dram_b = Buf()
    ys_d = nc.dram_tensor("ys_s", [D, SEQ], F32, kind=skind).ap()
    ys_v = ys_d.rearrange("(c p) t -> p c t", p=128)
    ys_dram_b = [Buf() for _ in range(KC)]
    mA_dram_b = [Buf() for _ in range(NH)]
    h1_dram_b = [Buf() for _ in range(NT)]
    u2T_dram_b = [Buf() for _ in range(NTB)]

    w_in_v = w_in_d.rearrange("(kc p) n -> p kc n", p=128)
    w_co_v = w_co_d.rearrange("(kc p) n -> p kc n", p=128)
    w_out_v = w_out_d.rearrange("(kc p) n -> p kc n", p=128)
    w_ff1_v = w_ff1_d.rearrange("(kc p) n -> p kc n", p=128)
    w_ff2_v = w_ff2_d.rearrange("(jg kc p) n -> p jg kc n", p=128, kc=16)
    mA_v = mA_d.rearrange("(c p) t -> p c t", p=128)
    u2T_v = u2T_d.rearrange("(c p) t -> p c t", p=128)

    scope = [stack]
    import os

    def _rem(tag):
        if os.environ.get("KDBG"):
            print("SBUF remaining", tag, nc.sbuf_bytes_remaining, "sems", sc.nsem, {k: sc.cnt[k] for k in sc.cnt})

    def sb(name, shape, dt):
        return scope[0].enter_context(nc.sbuf_tensor("sb_" + name, list(shape), dt))

    def barrier():
        toks = [(sc.sem[k], sc.cnt[k]) for k in sc.eng if sc.cnt[k] > 0]
        toks += [(b.sem, b.cnt) for b in sc.dbufs if b.cnt > 0]
        for k in sc.eng:
            sc._wait(k, toks)
    class _Stop(Exception):
        pass

    def checkpoint(name):
        if stop == name:
            barrier()
            raise _Stop()
    ident = sb("ident", [128, 128], BF16)
    rotm = sb("rotm", [128, 128], BF16)
    maskT = sb("maskT", [128, 2, 128], BF16)
    cols = sb("cols", [128, NCOL], F32)
    ncols = sb("ncols", [128, NCOL], F32)
    gsub = sb("gsub", [128, 2], F32)
    dwk = sb("dwk", [128, KC * CONV_W], F32)
    lamt = sb("lamt", [128, 4, 128], F32)
    lamc = sb("lamc", [128, 8], F32)
    ones_f = sb("ones_f", [128, 128], F32)
    dummy = sb("dummy", [128, 8], F32)
    sm = sb("sm", [128, 64], F32)
    b_const = Buf()
    b_dummy = Buf()

    banks = [nc.alloc_psum_tensor(f"bank{i}", [128, 512], F32) for i in range(8)]
    bank_b = [Buf() for _ in range(8)]
    for _b in bank_b:
        _b.excl = True
    banks_bf = [b.bitcast(BF16) for b in banks]

    sm_ctr = [0]
    sm_bufs = [Buf() for _ in range(64)]

    def sm_alloc():
        i = sm_ctr[0] % 64
        sm_ctr[0] += 1
        return sm[:, i:i + 1], sm_bufs[i]

    sc.op("dve", lambda e: e.memset(dummy[:], 0.0), [], [b_dummy])
    sc.op("act", lambda e: e.activation(out=dummy[:, 0:4], in_=dummy[:, 4:8], func=AF.Copy), [b_dummy], [b_dummy])

    cb = sc.dmabuf()
    cbp = sc.dmabuf()
    for dst, src in ((ident[:], ident_d), (rotm[:], rot_d), (maskT[:], mask_d)):
        sc.dma("pool", dst, src, cbp, [], [b_const])
        sc._wait("pool", [b_const.w])
    sc.dma("sp", cols[:], cols_d, cb, [], [b_const])
    sc._wait("sp", [b_const.w])
    sc.dma("sp", dwk[:], dwk_d, cb, [], [b_const])
    sc._wait("sp", [b_const.w])
    for i in range(4):
        sc.dma("sp", lamt[:, i, :], lam_d[i:i + 1, :].broadcast_to([128, 128]), cb, [], [b_const])
        sc._wait("sp", [b_const.w])
    sc.op("dve", lambda e: e.memset(ones_f[:], 1.0), [], [b_const])
    sc.op("dve", lambda e: e.tensor_scalar(out=ncols[:], in0=cols[:], scalar1=-1.0, scalar2=None, op0=ALU.mult), [b_const], [b_const])
    sc.op("dve", lambda e: e.tensor_scalar(out=gsub[:], in0=cols[:, COL_SUB:COL_SUB + 2], scalar1=1.0 - LAM_INIT, scalar2=None, op0=ALU.mult), [b_const], [b_const])
    sc.op("dve", lambda e: e.tensor_tensor(out=lamt[:, 0, :], in0=lamt[:, 0, :], in1=lamt[:, 1, :], op=ALU.mult), [b_const], [b_const])
    sc.op("dve", lambda e: e.tensor_tensor(out=lamt[:, 2, :], in0=lamt[:, 2, :], in1=lamt[:, 3, :], op=ALU.mult), [b_const], [b_const])
    sc.op("dve", lambda e: e.reduce_sum(out=lamc[:, 0:1], in_=lamt[:, 0, :], axis=AX.X), [b_const], [b_const])
    sc.op("dve", lambda e: e.reduce_sum(out=lamc[:, 1:2], in_=lamt[:, 2, :], axis=AX.X), [b_const], [b_const])
    sc.op("act", lambda e: e.activation(out=lamc[:, 2:4], in_=lamc[:, 0:2], func=AF.Exp), [b_const, b_dummy], [b_const])
    sc.op("dve", lambda e: e.tensor_tensor(out=lamc[:, 4:5], in0=lamc[:, 2:3], in1=lamc[:, 3:4], op=ALU.subtract), [b_const], [b_const])
    sc.op("dve", lambda e: e.tensor_scalar(out=lamc[:, 5:6], in0=lamc[:, 4:5], scalar1=LAM_INIT, scalar2=None, op0=ALU.add), [b_const], [b_const])
    lam_ap = lamc[:, 5:6]
    if stop == "const":
        barrier()
        return nc, stack

    def rstd_chain(ss_ap, ss_buf, scale, eps):
        l_ap, l_buf = sm_alloc()
        r_ap, r_buf = sm_alloc()
        sc.op("act", lambda e: e.activation(out=l_ap, in_=ss_ap, func=AF.Ln, scale=scale, bias=eps), [ss_buf], [l_buf])
        sc.op("act", lambda e: e.activation(out=r_ap, in_=l_ap, func=AF.Exp, scale=-0.5), [l_buf], [r_buf])
        return r_ap, r_buf

    st12 = contextlib.ExitStack()
    scope[0] = st12
    uT = sb("uT", [128, KC, SEQ], BF16)
    uT_b = [Buf() for _ in range(NTB)]
    uT_st = sc.dmabuf()
    st1 = contextlib.ExitStack()
    scope[0] = st1

    gbc = sb("gbc", [128, D], F32)
    gbc_b = sc.dmabuf()
    sc.dma("sp", gbc[:], g_pre_d.broadcast_to([128, D]), gbc_b, [], [gbc_b])
    xt = [sb(f"xt{i}", [128, D], F32) for i in range(2)]
    xt_b = [sc.dmabuf() for _ in range(2)]
    ub = [sb(f"ub{i}", [128, D], BF16) for i in range(2)]
    ub_b = [Buf() for _ in range(2)]
    junk = sb("junk", [128, D], BF16)
    junk_b = Buf()

    def norm_to_T(src_ap, src_buf, g_ap, g_buf, slot, dstT, dst_buf, tcol, eps=1e-6):
        ss_ap, ss_b = sm_alloc()
        sc.op("act", lambda e: e.activation(out=junk[:], in_=src_ap, func=AF.Square, accum_out=ss_ap), [src_buf], [junk_b, ss_b])
        r_ap, r_b = rstd_chain(ss_ap, ss_b, 1.0 / D, eps)
        sc.op("dve", lambda e: e.scalar_tensor_tensor(out=ub[slot][:], in0=src_ap, scalar=r_ap, in1=g_ap, op0=ALU.mult, op1=ALU.mult),
              [src_buf, r_b, g_buf], [ub_b[slot]])
        for half in range(2):
            bk = 4 + 2 * slot + half
            pv = banks_bf[bk].reshape([128, 8, 128])

            def tr(e, half=half, pv=pv):
                ins = None
                for i in range(8):
                    kc = half * 8 + i
                    ins = e.transpose(pv[:, i, :], ub[slot][:, kc * 128:(kc + 1) * 128], ident[:])
                return ins
            sc.op("pe", tr, [ub_b[slot], b_const], [bank_b[bk]])
            eng = "act" if half == 0 else "dve"
            if eng == "act":
                sc.op("act", lambda e, half=half, pv=pv: e.activation(out=dstT[:, half * 8:half * 8 + 8, tcol:tcol + 128], in_=pv[:], func=AF.Copy),
                      [bank_b[bk]], [dst_buf])
            else:
                sc.op("dve", lambda e, half=half, pv=pv: e.tensor_copy(out=dstT[:, half * 8:half * 8 + 8, tcol:tcol + 128], in_=pv[:]),
                      [bank_b[bk]], [dst_buf])

    for tt in range(NT):
        s = tt % 2
        sc.dma("sp", xt[s][:], x_d[tt * 128:(tt + 1) * 128, :], xt_b[s], [], [xt_b[s]])
        norm_to_T(xt[s][:], xt_b[s], gbc[:], gbc_b, s, uT, uT_b[tt // 4], tt * 128)
    sc.dma("sp", uT_v, uT[:], uT_st, uT_b, [uT_dram_b])
    _rem("p1")
    barrier()
    if stop == "p1":
        return nc, stack
    st1.close()
    st2 = contextlib.ExitStack()
    scope[0] = st2

    WSLOT = 2
    wq = [sb(f"wq{i}", [128, KC, 512], BF16) for i in range(WSLOT)]
    wq_b = [sc.dmabuf() for _ in range(WSLOT)]
    wctr = [0]

    def wload(src_ap, shape3=None):
        i = wctr[0] % WSLOT
        wctr[0] += 1
        n = src_ap.shape[-1]
        sc.dma("pool", wq[i][:, :, 0:n], src_ap, wq_b[i], [], [wq_b[i]])
        return wq[i], wq_b[i]

    qT = sb("qT", [128, 2, SEQ], BF16)
    kT = sb("kT", [128, 2, SEQ], BF16)
    qT_b = [Buf() for _ in range(NTB)]
    kT_b = [Buf() for _ in range(NTB)]
    vaug = sb("vaug", [128, NT, 260], BF16)
    v_b = [Buf() for _ in range(NTB)]
    mAh = [sb(f"mAh{i}", [128, 2, SEQ], BF16) for i in range(2)]
    mAh_b = [[Buf() for _ in range(NTB)] for _ in range(2)]
    mAh_st = [sc.dmabuf() for _ in range(2)]
    cs = [sb(f"cs{i}", [128, 2, TB], F32) for i in range(2)]
    cs_b = [sc.dmabuf() for _ in range(2)]
    qraw = [sb(f"qraw{i}", [128, TB], BF16) for i in range(2)]
    qraw_b = [Buf() for _ in range(2)]
    rt1 = [sb(f"rt1_{i}", [128, TB], F32) for i in range(2)]
    rt1_b = [Buf() for _ in range(2)]
    rt2 = [sb(f"rt2_{i}", [128, TB], F32) for i in range(2)]
    rt2_b = [Buf() for _ in range(2)]
    gtmp = [sb(f"gtmp{i}", [128, TB], F32) for i in range(2)]
    gtmp_b = [Buf() for _ in range(2)]
    NE = 4
    et = [sb(f"et{i}", [128, 2, 256], BF16) for i in range(NE)]
    et_b = [Buf() for _ in range(NE)]
    ot = [sb(f"ot{i}", [128, 256], F32) for i in range(2)]
    ot_b = [Buf() for _ in range(2)]
    ot2 = [sb(f"ot2_{i}", [128, 256], F32) for i in range(2)]
    ot2_b = [Buf() for _ in range(2)]
    onb = [sb(f"onb{i}", [128, 256], BF16) for i in range(2)]
    onb_b = [Buf() for _ in range(2)]
    sc.op("dve", lambda e: e.memset(vaug[:, :, 256:260], 1.0), [], [v_b[0]])
    wcc = [sb(f"wcc{i}", [128, KC, 256], BF16) for i in range(2)]
    wcc_b = [sc.dmabuf() for _ in range(2)]
    gtc = [sb(f"gtc{i}", [128, HALO + TB], F32) for i in range(2)]
    gtc_b = [Buf() for _ in range(2)]
    sgc = sb("sgc", [128, TB], F32)
    sgc_b = Buf()
    accc = sb("accc", [128, TB], F32)
    accc_b = Buf()
    yst = [sb(f"yst{i}", [128, TB], F32) for i in range(2)]
    yst_b = [sc.dmabuf() for _ in range(2)]
    tailc = sb("tailc", [128, HALO], F32)
    tailc_b = Buf()
    cu_ctr = [0]
    aev = sb("aev", [128, 2, 2, 260], F32)
    aev_b = Buf()
    pending = []

    def conv_unit(c, tb):
        slot = c % 2
        t0 = tb * TB
        s2 = cu_ctr[0] % 2
        cu_ctr[0] += 1
        if tb == 0:
            sc.dma("pool", wcc[slot][:, :, 0:128], w_in_v[:, :, C_CV + c * 128:C_CV + (c + 1) * 128], wcc_b[slot], [], [wcc_b[slot]])
            sc.dma("pool", wcc[slot][:, :, 128:256], w_in_v[:, :, C_CG + c * 128:C_CG + (c + 1) * 128], wcc_b[slot], [], [wcc_b[slot]])
            sc.op("dve", lambda e: e.memset(tailc[:], 0.0), [], [tailc_b])
        wc, wc_b = wcc[slot], wcc_b[slot]

        def pjg(e):
            ins = None
            for kc in range(KC):
                ins = e.matmul(banks[7][:], lhsT=wc[:, kc, 128:256], rhs=uT[:, kc, t0:t0 + TB], start=(kc == 0), stop=(kc == KC - 1))
            return ins

        def pjv(e):
            ins = None
            for kc in range(KC):
                ins = e.matmul(banks[6][:], lhsT=wc[:, kc, 0:128], rhs=uT[:, kc, t0:t0 + TB], start=(kc == 0), stop=(kc == KC - 1))
            return ins
        sc.op("pe", pjg, [wc_b, uT_b[tb]], [bank_b[7]])
        sc.op("pe", pjv, [wc_b, uT_b[tb]], [bank_b[6]])
        sc.op("act", lambda e: e.activation(out=sgc[:], in_=banks[7][:], func=AF.Exp, scale=-1.0, bias=ncols[:, COL_BG + c:COL_BG + c + 1]),
              [bank_b[7], b_const], [sgc_b])
        sc.op("act", lambda e: e.activation(out=sgc[:], in_=sgc[:], func=AF.Ln, bias=1.0), [sgc_b], [sgc_b])
        sc.op("act", lambda e: e.activation(out=sgc[:], in_=sgc[:], func=AF.Exp, scale=-1.0), [sgc_b], [sgc_b])
        g_ = gtc[s2]
        sc.op("dve", lambda e: e.tensor_copy(out=g_[:, 0:HALO], in_=tailc[:]), [tailc_b], [gtc_b[s2]])
        sc.op("dve", lambda e: e.scalar_tensor_tensor(out=g_[:, HALO:HALO + TB], in0=banks[6][:], scalar=cols[:, COL_BV + c:COL_BV + c + 1],
                                                      in1=sgc[:], op0=ALU.add, op1=ALU.mult), [bank_b[6], sgc_b, b_const], [gtc_b[s2]])
        sc.op("dve", lambda e: e.tensor_copy(out=tailc[:], in_=g_[:, TB:TB + HALO]), [gtc_b[s2]], [tailc_b])

        def wk(j):
            return dwk[:, c * CONV_W + j:c * CONV_W + j + 1]
        bufA = [(accc[:], accc_b), (yst[s2][:], yst_b[s2])]
        bufB = [(rt1[0][:], rt1_b[0]), (rt1[1][:], rt1_b[1])]
        for i in range(16):
            ja = 2 * i
            oa, oa_b = bufA[i % 2]
            if i == 0:
                sc.op("dve", lambda e, oa=oa: e.tensor_scalar(out=oa, in0=g_[:, 0:TB], scalar1=wk(0), scalar2=cols[:, COL_DWB + c:COL_DWB + c + 1],
                                                             op0=ALU.mult, op1=ALU.add), [gtc_b[s2], b_const], [oa_b])
            else:
                ia, ia_b = bufA[(i - 1) % 2]
                sc.op("dve", lambda e, oa=oa, ia=ia, ja=ja: e.scalar_tensor_tensor(out=oa, in0=g_[:, ja:ja + TB], scalar=wk(ja), in1=ia, op0=ALU.mult, op1=ALU.add),
                      [gtc_b[s2], b_const, ia_b], [oa_b])
            if i < 15:
                jb = 2 * i + 1
                ob, ob_b = bufB[i % 2]
                if i == 0:
                    sc.op("dve", lambda e, ob=ob, jb=jb: e.tensor_scalar(out=ob, in0=g_[:, jb:jb + TB], scalar1=wk(jb), scalar2=None, op0=ALU.mult),
                          [gtc_b[s2], b_const], [ob_b])
                else:
                    ib, ib_b = bufB[(i - 1) % 2]
                    sc.op("dve", lambda e, ob=ob, ib=ib, jb=jb: e.scalar_tensor_tensor(out=ob, in0=g_[:, jb:jb + TB], scalar=wk(jb), in1=ib, op0=ALU.mult, op1=ALU.add),
                          [gtc_b[s2], b_const, ib_b], [ob_b])
        yf, yf_b = bufB[1]
        sc.op("dve", lambda e: e.tensor_tensor(out=yf, in0=yst[s2][:], in1=bufB[0][0], op=ALU.add), [yst_b[s2], bufB[0][1]], [yf_b])
        sc.dma("sp", ys_v[:, c, t0:t0 + TB], yf, yst_b[s2], [yf_b], [ys_dram_b[c]])

    pj_ctr = [0]
    ectr = [0]
    fin_ctr = [0]

    for h in range(0 if os.environ.get('KSKIP') else NH):
        sc.rotate()
        hs = h % 2
        wQ, wQ_b = None, None
        wqk, wqk_b = wload(w_in_v[:, :, C_Q + h * 256:C_Q + (h + 1) * 256])
        iqk = (wctr[0] - 1) % WSLOT
        sc.dma("pool", wq[iqk][:, :, 256:512], w_in_v[:, :, C_K + h * 256:C_K + (h + 1) * 256], wq_b[iqk], [], [wq_b[iqk]])
        wvg, wvg_b = wload(w_in_v[:, :, C_V + h * 256:C_V + (h + 1) * 256])
        ivg = (wctr[0] - 1) % WSLOT
        sc.dma("pool", wq[ivg][:, :, 256:512], w_in_v[:, :, C_GA + h * 256:C_GA + (h + 1) * 256], wq_b[ivg], [], [wq_b[ivg]])

        for tb in range(NTB):
            t0 = tb * TB
            cslot = (h * NTB + tb) % 2
            sc.dma("sp", cs[cslot][:, 0, :], cos_d[:, t0:t0 + TB], cs_b[cslot], [], [cs_b[cslot]])
            sc.dma("sp", cs[cslot][:, 1, :], sin_d[:, t0:t0 + TB], cs_b[cslot], [], [cs_b[cslot]])
            units = [(typ, sub) for typ in (0, 1) for sub in (0, 1)]
            pend = []

            def rope_tail(item):
                typ, sub, bk, rs = item
                dstT, dst_b = (qT, qT_b[tb]) if typ == 0 else (kT, kT_b[tb])
                rb = 4 + rs
                sc.op("pe", lambda e: e.matmul(banks[rb][:], lhsT=rotm[:], rhs=qraw[rs][:], start=True, stop=True),
                      [qraw_b[rs], b_const], [bank_b[rb]])
                sc.op("dve", lambda e: e.tensor_tensor(out=rt2[rs][:], in0=banks[rb][:], in1=cs[cslot][:, 1, :], op=ALU.mult),
                      [bank_b[rb], cs_b[cslot]], [rt2_b[rs]])
                sc.op("pool", lambda e: e.tensor_tensor(out=dstT[:, sub, t0:t0 + TB], in0=rt1[rs][:], in1=rt2[rs][:], op=ALU.add),
                      [rt1_b[rs], rt2_b[rs]], [dst_b])

            for (typ, sub) in units:
                bk = pj_ctr[0] % 4
                rs = pj_ctr[0] % 2
                pj_ctr[0] += 1
                col0 = typ * 256 + sub * 128
                scl = (128 ** -0.5) if typ == 0 else 1.0

                def proj(e, col0=col0, bk=bk):
                    ins = None
                    for kc in range(KC):
                        ins = e.matmul(banks[bk][:], lhsT=wqk[:, kc, col0:col0 + 128], rhs=uT[:, kc, t0:t0 + TB],
                                       start=(kc == 0), stop=(kc == KC - 1))
                    return ins
                sc.op("pe", proj, [wqk_b, uT_b[tb]], [bank_b[bk]])
                sc.op("act", lambda e, bk=bk, rs=rs, scl=scl: e.activation(out=qraw[rs][:], in_=banks[bk][:], func=AF.Copy, scale=scl),
                      [bank_b[bk]], [qraw_b[rs]])
                sc.op("dve", lambda e, bk=bk, rs=rs, scl=scl: e.scalar_tensor_tensor(out=rt1[rs][:], in0=banks[bk][:], scalar=scl, in1=cs[cslot][:, 0, :],
                                                                                op0=ALU.mult, op1=ALU.mult),
                      [bank_b[bk], cs_b[cslot]], [rt1_b[rs]])
                pend.append((typ, sub, bk, rs))
                if len(pend) > 1:
                    rope_tail(pend.pop(0))
            for c in range(2):
                bk = pj_ctr[0] % 4
                gs = pj_ctr[0] % 2
                pj_ctr[0] += 1

                def projg(e, c=c, bk=bk):
                    ins = None
                    for kc in range(KC):
                        ins = e.matmul(banks[bk][:], lhsT=wvg[:, kc, 256 + c * 128:256 + (c + 1) * 128], rhs=uT[:, kc, t0:t0 + TB],
                                       start=(kc == 0), stop=(kc == KC - 1))
                    return ins
                sc.op("pe", projg, [wvg_b, uT_b[tb]], [bank_b[bk]])
                if pend:
                    rope_tail(pend.pop(0))
                sc.op("act", lambda e, bk=bk, gs=gs: e.activation(out=gtmp[gs][:], in_=banks[bk][:], func=AF.Exp, scale=-1.0),
                      [bank_b[bk]], [gtmp_b[gs]])
                sc.op("act", lambda e, gs=gs: e.activation(out=gtmp[gs][:], in_=gtmp[gs][:], func=AF.Ln, bias=1.0),
                      [gtmp_b[gs]], [gtmp_b[gs]])
                sc.op("act", lambda e, gs=gs, c=c: e.activation(out=mAh[hs][:, c, t0:t0 + TB], in_=gtmp[gs][:], func=AF.Exp, scale=-1.0),
                      [gtmp_b[gs]], [mAh_b[hs][tb]])
            while pend:
                rope_tail(pend.pop(0))
            for ti in range(4):
                tt = tb * 4 + ti
                bk = pj_ctr[0] % 4
                pj_ctr[0] += 1

                def projv(e, tt=tt, bk=bk):
                    ins = None
                    for kc in range(KC):
                        ins = e.matmul(banks[bk][:, 0:256], lhsT=uT[:, kc, tt * 128:(tt + 1) * 128], rhs=wvg[:, kc, 0:256],
                                       start=(kc == 0), stop=(kc == KC - 1))
                    return ins
                sc.op("pe", projv, [wvg_b, uT_b[tb]], [bank_b[bk]])
                sc.op("act", lambda e, tt=tt, bk=bk: e.activation(out=vaug[:, tt, 0:256], in_=banks[bk][:, 0:256], func=AF.Copy),
                      [bank_b[bk]], [v_b[tb]])

        for g in range(8):
            conv_unit(2 * h + g // 4, g % 4)
            qb0 = 2 * g
            nj = 2 * g + 2
            qtb = (g * 256) // TB
            pendq = []

            def av(j, es):
                for qi in range(2):
                    qb = qb0 + qi
                    if j > qb:
                        continue
                    for sub in range(2):
                        ab = qi * 2 + sub
                        sc.op("pe", lambda e, qi=qi, sub=sub, ab=ab: e.matmul(banks[ab][:, 0:257], lhsT=et[es][:, sub, qi * 128:(qi + 1) * 128],
                                                                              rhs=vaug[:, j, 0:257], start=(j == 0), stop=(j == qb)),
                              [et_b[es], v_b[j // 4]], [bank_b[ab]])

            for j in range(nj):
                sbk = 4 + (ectr[0] % 2)
                es = ectr[0] % NE
                ectr[0] += 1
                sv = banks[sbk].reshape([128, 2, 256])

                def qk(e, j=j, sv=sv):
                    ins = None
                    for sub in range(2):
                        lk = kT[:, sub, j * 128:(j + 1) * 128]
                        if j < qb0:
                            ins = e.matmul(sv[:, sub, :], lhsT=lk, rhs=qT[:, sub, g * 256:(g + 1) * 256], start=True, stop=True)
                            continue
                        qi = j - qb0
                        e.matmul(sv[:, sub, qi * 128:(qi + 1) * 128], lhsT=lk, rhs=qT[:, sub, (qb0 + qi) * 128:(qb0 + qi + 1) * 128], start=True, stop=False)
                        ins = e.matmul(sv[:, sub, qi * 128:(qi + 1) * 128], lhsT=ident[:], rhs=maskT[:, 1, :], start=False, stop=True)
                        if qi == 0:
                            ins = e.matmul(sv[:, sub, 128:256], lhsT=lk, rhs=qT[:, sub, (qb0 + 1) * 128:(qb0 + 2) * 128], start=True, stop=True)
                    return ins
                sc.op("pe", qk, [kT_b[j // 4], qT_b[qtb]], [bank_b[sbk]])
                sc.op("act", lambda e, sv=sv, es=es: e.activation(out=et[es][:], in_=sv[:], func=AF.Exp), [bank_b[sbk]], [et_b[es]])
                pendq.append((j, es))
                if len(pendq) > 1:
                    av(*pendq.pop(0))
            while pendq:
                av(*pendq.pop(0))
            for fn_ in pending:
                fn_()
            pending.clear()
            for qi in range(2):
                for sub in range(2):
                    ab = qi * 2 + sub
                    if sub == 0:
                        sc.op("act", lambda e, qi=qi, sub=sub, ab=ab: e.activation(out=aev[:, qi, sub, 0:257], in_=banks[ab][:, 0:257], func=AF.Copy),
                              [bank_b[ab]], [aev_b])
                    else:
                        sc.op("dve", lambda e, qi=qi, sub=sub, ab=ab: e.tensor_copy(out=aev[:, qi, sub, 0:257], in_=banks[ab][:, 0:257]),
                              [bank_b[ab]], [aev_b])

            def finalize(h=h, hs=hs, qb0=qb0):
                for qi in range(2):
                    qb = qb0 + qi
                    fs = fin_ctr[0] % 2
                    fin_ctr[0] += 1
                    a1, a2 = aev[:, qi, 0, :], aev[:, qi, 1, :]
                    r1, r1_b = sm_alloc()
                    r2, r2_b = sm_alloc()
                    sc.op("dve", lambda e: e.reciprocal(out=r1, in_=a1[:, 256:257]), [aev_b], [r1_b])
                    sc.op("dve", lambda e: e.reciprocal(out=r2, in_=a2[:, 256:257]), [aev_b], [r2_b])
                    sc.op("dve", lambda e: e.tensor_tensor(out=r2, in0=r2, in1=lam_ap, op=ALU.mult), [r2_b, b_const], [r2_b])
                    sc.op("dve", lambda e: e.tensor_scalar(out=ot2[fs][:], in0=a2[:, 0:256], scalar1=r2, scalar2=None, op0=ALU.mult),
                          [aev_b, r2_b], [ot2_b[fs]])
                    sc.op("dve", lambda e: e.scalar_tensor_tensor(out=ot[fs][:], in0=a1[:, 0:256], scalar=r1, in1=ot2[fs][:], op0=ALU.mult, op1=ALU.subtract),
                          [aev_b, r1_b, ot2_b[fs]], [ot_b[fs]])
                    ss, ss_b = sm_alloc()
                    sc.op("act", lambda e: e.activation(out=ot2[fs][:], in_=ot[fs][:], func=AF.Square, accum_out=ss), [ot_b[fs], ot2_b[fs]], [ot2_b[fs], ss_b])
                    rs_, rs_b = rstd_chain(ss, ss_b, 1.0 / 256, 1e-5)
                    sc.op("dve", lambda e: e.tensor_scalar(out=onb[fs][:], in0=ot[fs][:], scalar1=rs_, scalar2=None, op0=ALU.mult),
                          [ot_b[fs], rs_b], [onb_b[fs]])
                    pT = banks_bf[7].reshape([128, 8, 128])

                    def tr2(e):
                        ins = None
                        for c in range(2):
                            ins = e.transpose(pT[:, fs * 2 + c, :], onb[fs][:, c * 128:(c + 1) * 128], ident[:])
                        return ins
                    sc.op("pe", tr2, [onb_b[fs], b_const], [bank_b[7]])
                    mtb = (qb * 128) // TB
                    for c in range(2):
                        dst = mAh[hs][:, c, qb * 128:(qb + 1) * 128]
                        sc.op("dve", lambda e, c=c, dst=dst: e.scalar_tensor_tensor(out=dst, in0=pT[:, fs * 2 + c, :], scalar=gsub[:, c:c + 1], in1=dst,
                                                                                  op0=ALU.mult, op1=ALU.mult),
                              [bank_b[7], b_const, mAh_b[hs][mtb]], [mAh_b[hs][mtb]])
            pending.append(finalize)

        def store(h=h, hs=hs):
            sc.dma("sp", mA_v[:, 2 * h:2 * h + 2, :], mAh[hs][:], mAh_st[hs], [mAh_b[hs][i] for i in range(NTB)], [mA_dram_b[h]])
        pending.append(store)
        if stop == "p2h0":
            for fn_ in pending:
                fn_()
            barrier()
            return nc, stack

    for fn_ in pending:
        fn_()
    pending.clear()
    mA_done = mA_dram_b
    _rem("p2")
    barrier()
    if stop == "p2":
        return nc, stack
    st2.close()
    st12.close()
    st3 = contextlib.ExitStack()
    scope[0] = st3
    WSLOT = 3
    wq = [sb(f"wq3_{i}", [128, KC, 512], BF16) for i in range(WSLOT)]
    wq_b = [sc.dmabuf() for _ in range(WSLOT)]
    junk = sb("junk3", [128, D], BF16)
    junk_b = Buf()
    ub = [sb(f"ub3_{i}", [128, D], BF16) for i in range(2)]
    ub_b = [Buf() for _ in range(2)]
    xt = [sb(f"gb3_{i}", [128, D], F32) for i in range(2)]
    xt_b = [sc.dmabuf() for _ in range(2)]
    uTb = sb("uTb", [128, KC, TB], BF16)
    uTb_b = sc.dmabuf()

    gpm = xt[0]
    gpf = xt[1]
    sc.dma("sp", gpm[:], g_pm_d.broadcast_to([128, D]), xt_b[0], [], [xt_b[0]])
    sc.dma("sp", gpf[:], g_pf_d.broadcast_to([128, D]), xt_b[1], [], [xt_b[1]])
    gpm_b, gpf_b = xt_b[0], xt_b[1]
    ybuf = sb("ybuf", [128, KC, TB], F32)
    y_b = [Buf() for _ in range(KC)]
    yld_b = sc.dmabuf()
    cT = sb("cT", [128, KC, TB], BF16)
    cT_b = sc.dmabuf()
    mblk = sb("mblk", [128, KC, TB], BF16)
    mblk_b = sc.dmabuf()
    acc2 = [sb(f"acc2_{i}", [128, TB], F32) for i in range(2)]
    acc2_b = [Buf() for _ in range(2)]
    ysq = [sb(f"ysq{i}", [128, TB], F32) for i in range(2)]
    ysq_b = [Buf() for _ in range(2)]
    sgm = [sb(f"sgm{i}", [128, TB], F32) for i in range(2)]
    sgm_b = [Buf() for _ in range(2)]
    mean_t = sb("mean_t", [128, TB], F32)
    rstd_t = sb("rstd_t", [128, TB], F32)
    stat_b = Buf()
    lt1, lt1_b = acc2, acc2_b
    lt2, lt2_b = ysq, ysq_b
    xr = [sb("xr0", [128, D], F32)] * 2
    xr_b = [sc.dmabuf()] * 2
    h1t = [sb(f"h1t{i}", [128, D], F32) for i in range(2)]
    h1t_b = [sc.dmabuf() for _ in range(2)]
    u2s, u2s_b = cT, cT_b

    pc = [0]
    for tb in range(0 if os.environ.get('KSKIP') else NTB):
        sc.rotate()
        t0 = tb * TB
        sc.dma("sp", mblk[:], mA_v[:, :, t0:t0 + TB], mblk_b, mA_done, [mblk_b])
        sc.dma("sp", uTb[:], uT_v[:, :, t0:t0 + TB], uTb_b, [uT_dram_b], [uTb_b])
        sc.dma("sp", ybuf[:], ys_v[:, :, t0:t0 + TB], yld_b, ys_dram_b, y_b)
        for c in range(KC):
            s2 = c % 2
            yv = ybuf[:, c, :]
            sc.op("act", lambda e, s2=s2, yv=yv: e.activation(out=ysq[s2][:], in_=yv, func=AF.Square), [y_b[c]], [ysq_b[s2]])
            sc.op("pe", lambda e, yv=yv, c=c: e.matmul(banks[4][:], lhsT=ones_f[:], rhs=yv, start=(c == 0), stop=(c == KC - 1)), [y_b[c], b_const], [bank_b[4]])
            sc.op("pe", lambda e, s2=s2, c=c: e.matmul(banks[5][:], lhsT=ones_f[:], rhs=ysq[s2][:], start=(c == 0), stop=(c == KC - 1)), [ysq_b[s2], b_const], [bank_b[5]])
        sc.op("dve", lambda e: e.tensor_scalar(out=mean_t[:], in0=banks[4][:], scalar1=1.0 / D, scalar2=None, op0=ALU.mult), [bank_b[4]], [stat_b])
        sc.op("dve", lambda e: e.tensor_tensor(out=lt1[0][:], in0=mean_t[:], in1=mean_t[:], op=ALU.mult), [stat_b], [lt1_b[0]])
        sc.op("dve", lambda e: e.scalar_tensor_tensor(out=rstd_t[:], in0=banks[5][:], scalar=1.0 / D, in1=lt1[0][:], op0=ALU.mult, op1=ALU.subtract),
              [bank_b[5], lt1_b[0]], [stat_b])
        sc.op("act", lambda e: e.activation(out=rstd_t[:], in_=rstd_t[:], func=AF.Ln, bias=1e-5), [stat_b], [stat_b])
        sc.op("act", lambda e: e.activation(out=rstd_t[:], in_=rstd_t[:], func=AF.Exp, scale=-0.5), [stat_b], [stat_b])
        for c in range(KC):
            s2 = c % 2
            yv = ybuf[:, c, :]
            sc.op("dve", lambda e, s2=s2, yv=yv: e.tensor_tensor(out=lt1[s2][:], in0=yv, in1=mean_t[:], op=ALU.subtract), [y_b[c], stat_b], [lt1_b[s2]])
            sc.op("dve", lambda e, s2=s2: e.tensor_tensor(out=lt2[s2][:], in0=lt1[s2][:], in1=rstd_t[:], op=ALU.mult), [lt1_b[s2], stat_b], [lt2_b[s2]])
            sc.op("act", lambda e, s2=s2, c=c: e.activation(out=lt1[s2][:], in_=lt2[s2][:], func=AF.Exp, scale=ncols[:, COL_LNG + c:COL_LNG + c + 1],
                                                          bias=ncols[:, COL_LNB + c:COL_LNB + c + 1]), [lt2_b[s2], b_const], [lt1_b[s2]])
            sc.op("act", lambda e, s2=s2, c=c: e.activation(out=lt2[s2][:], in_=lt2[s2][:], func=AF.Identity, scale=cols[:, COL_LNG + c:COL_LNG + c + 1],
                                                          bias=cols[:, COL_LNB + c:COL_LNB + c + 1]), [lt2_b[s2], b_const], [lt2_b[s2]])
            sc.op("act", lambda e, s2=s2: e.activation(out=lt1[s2][:], in_=lt1[s2][:], func=AF.Ln, bias=1.0), [lt1_b[s2]], [lt1_b[s2]])
            sc.op("act", lambda e, s2=s2: e.activation(out=lt1[s2][:], in_=lt1[s2][:], func=AF.Exp, scale=-1.0), [lt1_b[s2]], [lt1_b[s2]])
            sc.op("dve", lambda e, s2=s2, c=c: e.tensor_tensor(out=cT[:, c, :], in0=lt2[s2][:], in1=lt1[s2][:], op=ALU.mult), [lt1_b[s2], lt2_b[s2]], [cT_b])
        for fg in range(4):
            wco, wco_b = wload(w_co_v[:, :, fg * 512:(fg + 1) * 512])
            wgc, wgc_b = wload(w_in_v[:, :, C_GC + fg * 512:C_GC + (fg + 1) * 512])
            for fi in range(4):
                f = fg * 4 + fi
                s2 = pc[0] % 2
                pc[0] += 1
                bo, bg = s2 * 2, s2 * 2 + 1

                def pco(e, fi=fi, bo=bo):
                    ins = None
                    for kc in range(KC):
                        ins = e.matmul(banks[bo][:], lhsT=wco[:, kc, fi * 128:(fi + 1) * 128], rhs=cT[:, kc, :], start=(kc == 0), stop=(kc == KC - 1))
                    return ins

                def pgc(e, fi=fi, bg=bg):
                    ins = None
                    for kc in range(KC):
                        ins = e.matmul(banks[bg][:], lhsT=wgc[:, kc, fi * 128:(fi + 1) * 128], rhs=uTb[:, kc, :], start=(kc == 0), stop=(kc == KC - 1))
                    return ins
                sc.op("pe", pgc, [wgc_b, uTb_b], [bank_b[bg]])
                sc.op("pe", pco, [wco_b, cT_b], [bank_b[bo]])
                sc.op("act", lambda e, bg=bg, s2=s2: e.activation(out=sgm[s2][:], in_=banks[bg][:], func=AF.Exp, scale=-1.0), [bank_b[bg]], [sgm_b[s2]])
                sc.op("act", lambda e, s2=s2: e.activation(out=sgm[s2][:], in_=sgm[s2][:], func=AF.Ln, bias=1.0), [sgm_b[s2]], [sgm_b[s2]])
                sc.op("act", lambda e, s2=s2: e.activation(out=sgm[s2][:], in_=sgm[s2][:], func=AF.Exp, scale=-1.0), [sgm_b[s2]], [sgm_b[s2]])
                sc.op("dve", lambda e, s2=s2, bo=bo, f=f: e.scalar_tensor_tensor(out=acc2[s2][:], in0=banks[bo][:], scalar=cols[:, COL_BCO + f:COL_BCO + f + 1], in1=sgm[s2][:],
                                                                               op0=ALU.add, op1=ALU.mult), [bank_b[bo], sgm_b[s2], b_const], [acc2_b[s2]])
                sc.op("dve", lambda e, s2=s2, f=f: e.tensor_tensor(out=mblk[:, f, :], in0=acc2[s2][:], in1=mblk[:, f, :], op=ALU.add), [acc2_b[s2], mblk_b], [mblk_b])
        o2 = ybuf.reshape([128, 4, D])
        for nb in range(4):
            wo, wo_b = wload(w_out_v[:, :, nb * 512:(nb + 1) * 512])
            for ti in range(4):
                bk = ti

                def pwo(e, ti=ti, bk=bk):
                    ins = None
                    for kc in range(KC):
                        ins = e.matmul(banks[bk][:], lhsT=mblk[:, kc, ti * 128:(ti + 1) * 128], rhs=wo[:, kc, :], start=(kc == 0), stop=(kc == KC - 1))
                    return ins
                sc.op("pe", pwo, [wo_b, mblk_b, cT_b], [bank_b[bk]])
                if ti % 2 == 0:
                    sc.op("act", lambda e, ti=ti, bk=bk, nb=nb: e.activation(out=o2[:, ti, nb * 512:(nb + 1) * 512], in_=banks[bk][:], func=AF.Copy), [bank_b[bk]], y_b)
                else:
                    sc.op("dve", lambda e, ti=ti, bk=bk, nb=nb: e.tensor_copy(out=o2[:, ti, nb * 512:(nb + 1) * 512], in_=banks[bk][:]), [bank_b[bk]], y_b)
        for ti in range(4):
            tt = tb * 4 + ti
            s2 = tt % 2
            sc.dma("sp", xr[s2][:], x_d[tt * 128:(tt + 1) * 128, :], xr_b[s2], [], [xr_b[s2]])
            ss, ss_b = sm_alloc()
            sc.op("act", lambda e, ti=ti: e.activation(out=junk[:], in_=o2[:, ti, :], func=AF.Square, accum_out=ss), y_b, [junk_b, ss_b])
            r_ap, r_b = rstd_chain(ss, ss_b, 1.0 / D, 1e-6)
            sc.op("dve", lambda e, ti=ti, s2=s2: e.scalar_tensor_tensor(out=h1t[s2][:], in0=o2[:, ti, :], scalar=r_ap, in1=gpm[:], op0=ALU.mult, op1=ALU.mult),
                  y_b + [r_b, gpm_b], [h1t_b[s2]])
            sc.op("pool", lambda e, s2=s2: e.tensor_tensor(out=h1t[s2][:], in0=h1t[s2][:], in1=xr[s2][:], op=ALU.add), [h1t_b[s2], xr_b[s2]], [h1t_b[s2]])
            sc.dma("sp", h1_d[tt * 128:(tt + 1) * 128, :], h1t[s2][:], h1t_b[s2], [h1t_b[s2]], [h1_dram_b[tt]])
            norm_to_T(h1t[s2][:], h1t_b[s2], gpf[:], gpf_b, s2, u2s, u2s_b, ti * 128)
        sc.dma("sp", u2T_v[:, :, t0:t0 + TB], u2s[:], u2s_b, [u2s_b], [u2T_dram_b[tb]])

    _rem("p3")
    barrier()
    if stop == "p3":
        return nc, stack
    st3.close()
    st4 = contextlib.ExitStack()
    scope[0] = st4
    wq = [sb(f"wq4_{i}", [128, KC, 512], BF16) for i in range(WSLOT)]
    wq_b = [sc.dmabuf() for _ in range(WSLOT)]
    junk = sb("junk4", [128, D], BF16)
    junk_b = Buf()
    xt = [sb("gb4", [128, D], F32)]
    xt_b = [sc.dmabuf()]
    xr = [sb("xr4_0", [128, D], F32)] * 2
    xr_b = [sc.dmabuf()] * 2
    h1t = [sb(f"h1t4_{i}", [128, D], F32) for i in range(2)]
    h1t_b = [sc.dmabuf() for _ in range(2)]
    fo_t = sb("fo_t", [128, 4, D], F32)
    y_b = [Buf()]
    hT = sb("hT", [128, 64, TB], BF16)
    hT_b = [Buf() for _ in range(16)]
    u2b = sb("u2b", [128, KC, TB], BF16)
    u2b_b = sc.dmabuf()
    rl = [sb(f"rl{i}", [128, TB], BF16) for i in range(2)]
    rl_b = [Buf() for _ in range(2)]
    fo = fo_t
    gpo = xt[0]
    sc.dma("sp", gpo[:], g_po_d.broadcast_to([128, D]), xt_b[0], [], [xt_b[0]])
    gpo_b = xt_b[0]
    outs = []
    fc = [0]
    for tb in range(NTB):
        sc.rotate()
        t0 = tb * TB
        sc.dma("sp", u2b[:], u2T_v[:, :, t0:t0 + TB], u2b_b, [u2T_dram_b[tb]], [u2b_b])
        for jb in range(16):
            w1, w1_b = wload(w_ff1_v[:, :, jb * 512:(jb + 1) * 512])
            for jc in range(4):
                j = jb * 4 + jc
                s2 = fc[0] % 2
                fc[0] += 1
                bk = 4 + s2

                def pf1(e, jc=jc, bk=bk):
                    ins = None
                    for kc in range(KC):
                        ins = e.matmul(banks[bk][:], lhsT=w1[:, kc, jc * 128:(jc + 1) * 128], rhs=u2b[:, kc, :], start=(kc == 0), stop=(kc == KC - 1))
                    return ins
                sc.op("pe", pf1, [w1_b, u2b_b], [bank_b[bk]])
                sc.op("act", lambda e, bk=bk, s2=s2: e.activation(out=rl[s2][:], in_=banks[bk][:], func=AF.Relu), [bank_b[bk]], [rl_b[s2]])
                sc.op("dve", lambda e, s2=s2, j=j: e.tensor_tensor(out=hT[:, j, :], in0=rl[s2][:], in1=rl[s2][:], op=ALU.mult), [rl_b[s2]], [hT_b[jb]])
        if stop == "p4a":
            barrier()
            return nc, stack
        for nb in range(4):
            for jg in range(4):
                w2, w2_b = wload(w_ff2_v[:, jg, :, nb * 512:(nb + 1) * 512])
                for ti in range(4):
                    def pf2(e, ti=ti, jg=jg):
                        ins = None
                        for kc in range(KC):
                            ins = e.matmul(banks[ti][:], lhsT=hT[:, jg * 16 + kc, ti * 128:(ti + 1) * 128], rhs=w2[:, kc, :],
                                           start=(jg == 0 and kc == 0), stop=(jg == 3 and kc == KC - 1))
                        return ins
                    sc.op("pe", pf2, [w2_b] + hT_b[jg * 4:(jg + 1) * 4], [bank_b[ti]])
            for ti in range(4):
                if ti % 2 == 0:
                    sc.op("act", lambda e, ti=ti, nb=nb: e.activation(out=fo[:, ti, nb * 512:(nb + 1) * 512], in_=banks[ti][:], func=AF.Copy), [bank_b[ti]], y_b)
                else:
                    sc.op("dve", lambda e, ti=ti, nb=nb: e.tensor_copy(out=fo[:, ti, nb * 512:(nb + 1) * 512], in_=banks[ti][:]), [bank_b[ti]], y_b)
        if stop == "p4b":
            barrier()
            return nc, stack
        for ti in range(4):
            tt = tb * 4 + ti
            s2 = tt % 2
            sc.dma("sp", xr[s2][:], h1_d[tt * 128:(tt + 1) * 128, :], xr_b[s2], [h1_dram_b[tt]], [xr_b[s2]])
            ss, ss_b = sm_alloc()
            sc.op("act", lambda e, ti=ti: e.activation(out=junk[:], in_=fo[:, ti, :], func=AF.Square, accum_out=ss), y_b, [junk_b, ss_b])
            r_ap, r_b = rstd_chain(ss, ss_b, 1.0 / D, 1e-6)
            sc.op("dve", lambda e, ti=ti, s2=s2: e.scalar_tensor_tensor(out=h1t[s2][:], in0=fo[:, ti, :], scalar=r_ap, in1=gpo[:], op0=ALU.mult, op1=ALU.mult),
                  y_b + [r_b, gpo_b], [h1t_b[s2]])
            sc.op("pool", lambda e, s2=s2: e.tensor_tensor(out=h1t[s2][:], in0=h1t[s2][:], in1=xr[s2][:], op=ALU.add), [h1t_b[s2], xr_b[s2]], [h1t_b[s2]])
            outs.append(sc.dma("sp", out_d[tt * 128:(tt + 1) * 128, :], h1t[s2][:], h1t_b[s2], [h1t_b[s2]], []))
        if stop == "p4c":
            barrier()
            return nc, stack
    sc._wait("sp", outs)
    _rem("p4")
    barrier()
    st4.close()
    return nc, stack


def _host_consts():
    inv_freq = 1.0 / (10000.0 ** (np.arange(0, 128, 2, dtype=np.float32) / 128.0))
    pos = np.arange(SEQ, dtype=np.float32)
    ang = pos[None, :] * inv_freq[:, None]
    cos = np.cos(ang).astype(np.float32)
    sin = np.sin(ang).astype(np.float32)
    cosT = np.concatenate([cos, cos], 0)
    sinT = np.concatenate([-sin, sin], 0)
    ident = np.eye(128, dtype=np.float32)
    rot = np.zeros((128, 128), np.float32)
    for m in range(128):
        rot[(m + 64) % 128, m] = 1.0
    kq = (np.arange(128)[None, :] >= np.arange(128)[:, None]).astype(np.float32)
    mask = np.stack([kq, (kq - 1.0) * 30000.0], 1)
    return dict(cosT=np.ascontiguousarray(cosT), sinT=np.ascontiguousarray(sinT), ident=ident, rotm=rot, maskT=np.ascontiguousarray(mask))


def _colmajor(v, n):
    return np.asarray(v, np.float32).reshape(n, 128).T


_CACHE = {}


def kernel(x, pre_mix_gain, w_in, lambda_q1, lambda_k1, lambda_q2, lambda_k2, subln_gain, glu_bias, dw_kernel, dw_bias,
           conv_ln_gain, conv_ln_bias, w_conv_out, b_conv_out, w_out, post_mix_gain, pre_ff_gain, w_ff1, w_ff2, post_ff_gain,
           _debug=False, _cores=8, _stop=None, _trace=False):
    f = lambda a: np.ascontiguousarray(np.asarray(a, dtype=np.float32))
    x = f(x)
    cols = np.concatenate([
        _colmajor(f(glu_bias)[0, :2048], 16), _colmajor(f(glu_bias)[0, 2048:], 16), _colmajor(f(dw_bias)[0], 16),
        _colmajor(f(conv_ln_gain)[0], 16), _colmajor(f(conv_ln_bias)[0], 16), _colmajor(f(b_conv_out)[0], 16),
        _colmajor(f(subln_gain)[0], 2)], axis=1)
    dwk = f(dw_kernel)[0].reshape(CONV_W, 16, 128).transpose(2, 1, 0).reshape(128, 16 * CONV_W)
    lamv = np.concatenate([f(lambda_q1), f(lambda_k1), f(lambda_q2), f(lambda_k2)], 0)
    shared = dict(
        w_in=f(w_in)[0], w_conv_out=f(w_conv_out)[0], w_out=f(w_out)[0], w_ff1=f(w_ff1)[0], w_ff2=f(w_ff2)[0],
        pre_mix_gain=f(pre_mix_gain), post_mix_gain=f(post_mix_gain), pre_ff_gain=f(pre_ff_gain), post_ff_gain=f(post_ff_gain),
        cols=np.ascontiguousarray(cols), dwk=np.ascontiguousarray(dwk), lamv=np.ascontiguousarray(lamv), **_host_consts())
    nc, stack = build_nc(debug=_debug, stop=_stop)
    in_maps = [dict(shared, x=x[b]) for b in range(_cores)]
    res = run_bass_kernel_spmd(nc, in_maps, core_ids=list(range(_cores)), **({'trace': True} if _trace else {}))
    if _trace:
        print('EXEC_TIME_NS', res.exec_time_ns)
    if _debug:
        return res
    return np.stack([np.asarray(r["out"], dtype=np.float32) for r in res.results], 0)
```

```python
import contextlib
import math
import numpy as np
import concourse.bass as bass
import concourse.mybir as mybir
from concourse.bass_utils import run_bass_kernel_spmd

F32 = mybir.dt.float32
BF16 = mybir.dt.bfloat16
AF = mybir.ActivationFunctionType
ALU = mybir.AluOpType
AX = mybir.AxisListType

D = 2048
SEQ = 2048
NH = 8
KC = 16
NT = 16
TB = 512
NTB = SEQ // TB
DFF = 8192
C_Q, C_K, C_V, C_CV, C_CG, C_GA, C_GC = 0, 2048, 4096, 6144, 8192, 10240, 12288
IN_W = 14336
LAM_INIT = 0.8 - 0.6 * math.exp(-0.3 * 0)
CONV_W = 31
HALO = CONV_W - 1
COL_BV, COL_BG, COL_DWB, COL_LNG, COL_LNB, COL_BCO, COL_SUB = 0, 16, 32, 48, 64, 80, 96
NCOL = 98
DVE_TAPS = 23


class Buf:
    __slots__ = ("w", "r", "sem", "cnt", "excl")

    def __init__(self):
        self.excl = False
        self.w = None
        self.r = {}
        self.sem = None
        self.cnt = 0


class Sched:
    def __init__(self, nc, stack):
        self.nc = nc
        self.stack = stack
        self.eng = {"pe": nc.tensor, "dve": nc.vector, "act": nc.scalar, "pool": nc.gpsimd, "sp": nc.sync}
        self.sem = {}
        self.cnt = {}
        self.seen = {k: {} for k in self.eng}
        self.nsem = 0
        self.dbufs = []
        for k in self.eng:
            self._new_sem(k)

    def _new_sem(self, k):
        self.nsem += 1
        self.sem[k] = self.stack.enter_context(self.nc.semaphore(f"s{self.nsem}_{k}"))
        self.cnt[k] = 0

    def rotate(self):
        for k in self.eng:
            if self.cnt[k] > 6000:
                self._new_sem(k)

    def dmabuf(self):
        b = Buf()
        self.nsem += 1
        b.sem = self.stack.enter_context(self.nc.semaphore(f"d{self.nsem}"))
        self.dbufs.append(b)
        return b

    def _wait(self, k, toks):
        seen = self.seen[k]
        for sem, val in toks:
            if seen.get(sem.num, 0) >= val:
                continue
            self.eng[k].wait_ge(sem, val)
            seen[sem.num] = val

    def _deps(self, reads, writes):
        toks = []
        for b in reads:
            if b.w is not None:
                toks.append(b.w)
            if b.excl:
                toks.extend(b.r.values())
        for b in writes:
            if b.w is not None:
                toks.append(b.w)
            toks.extend(b.r.values())
        return toks

    def _commit(self, tok, reads, writes):
        for b in reads:
            old = b.r.get(tok[0].num)
            if old is None or old[1] < tok[1]:
                b.r[tok[0].num] = tok
        for b in writes:
            b.w = tok
            b.r = {}

    def op(self, k, fn, reads=(), writes=()):
        self._wait(k, self._deps(reads, writes))
        ins = fn(self.eng[k])
        self.cnt[k] += 1
        ins.then_inc(self.sem[k], 1)
        tok = (self.sem[k], self.cnt[k])
        self._commit(tok, reads, writes)
        return tok

    def dma(self, k, out, in_, holder, reads=(), writes=()):
        self._wait(k, self._deps(reads, writes))
        ins = self.eng[k].dma_start(out=out, in_=in_)
        holder.cnt += 16
        ins.then_inc(holder.sem, 16)
        tok = (holder.sem, holder.cnt)
        self._commit(tok, reads, writes)
        return tok


def build_nc(debug=False, stop=None):
    nc = bass.Bass("TRN2", target_bir_lowering=False)
    stack = contextlib.ExitStack()
    sc = Sched(nc, stack)

    def din(name, shape, dt=F32):
        return nc.dram_tensor(name, list(shape), dt, kind="ExternalInput").ap()

    x_d = din("x", [SEQ, D])
    w_in_d = din("w_in", [D, IN_W])
    w_co_d = din("w_conv_out", [D, D])
    w_out_d = din("w_out", [D, D])
    w_ff1_d = din("w_ff1", [D, DFF])
    w_ff2_d = din("w_ff2", [DFF, D])
    g_pre_d = din("pre_mix_gain", [1, D])
    g_pm_d = din("post_mix_gain", [1, D])
    g_pf_d = din("pre_ff_gain", [1, D])
    g_po_d = din("post_ff_gain", [1, D])
    cols_d = din("cols", [128, NCOL])
    dwk_d = din("dwk", [128, KC * CONV_W])
    lam_d = din("lamv", [4, 128])
    ident_d = din("ident", [128, 128])
    rot_d = din("rotm", [128, 128])
    mask_d = din("maskT", [128, 2, 128])
    cos_d = din("cosT", [128, SEQ])
    sin_d = din("sinT", [128, SEQ])
    out_d = nc.dram_tensor("out", [SEQ, D], F32, kind="ExternalOutput").ap()
    skind = "ExternalOutput" if debug else "Internal"
    mA_d = nc.dram_tensor("mA_s", [D, SEQ], BF16, kind=skind).ap()
    h1_d = nc.dram_tensor("h1_s", [SEQ, D], F32, kind=skind).ap()
    u2T_d = nc.dram_tensor("u2T_s", [D, SEQ], BF16, kind=skind).ap()
    uT_d = nc.dram_tensor("uT_s", [D, SEQ], BF16, kind=skind).ap()
    uT_v = uT_d.rearrange("(c p) t -> p c t", p=128)
    uT_dram_b = Buf()
    ys_d = nc.dram_tensor("ys_s", [D, SEQ], F32, kind=skind).ap()
    ys_v = ys_d.rearrange("(c p) t -> p c t", p=128)
    ys_dram_b = [Buf() for _ in range(KC)]
    mA_dram_b = [Buf() for _ in range(NH)]
    h1_dram_b = [Buf() for _ in range(NT)]
    u2T_dram_b = [Buf() for _ in range(NTB)]

    w_in_v = w_in_d.rearrange("(kc p) n -> p kc n", p=128)
    w_co_v = w_co_d.rearrange("(kc p) n -> p kc n", p=128)
    w_out_v = w_out_d.rearrange("(kc p) n -> p kc n", p=128)
    w_ff1_v = w_ff1_d.rearrange("(kc p) n -> p kc n", p=128)
    w_ff2_v = w_ff2_d.rearrange("(jg kc p) n -> p jg kc n", p=128, kc=16)
    mA_v = mA_d.rearrange("(c p) t -> p c t", p=128)
    u2T_v = u2T_d.rearrange("(c p) t -> p c t", p=128)

    scope = [stack]
    import os

    def _rem(tag):
        if os.environ.get("KDBG"):
            print("SBUF remaining", tag, nc.sbuf_bytes_remaining, "sems", sc.nsem, {k: sc.cnt[k] for k in sc.cnt})

    def sb(name, shape, dt):
        return scope[0].enter_context(nc.sbuf_tensor("sb_" + name, list(shape), dt))

    def barrier():
        toks = [(sc.sem[k], sc.cnt[k]) for k in sc.eng if sc.cnt[k] > 0]
        toks += [(b.sem, b.cnt) for b in sc.dbufs if b.cnt > 0]
        for k in sc.eng:
            sc._wait(k, toks)
    class _Stop(Exception):
        pass

    def checkpoint(name):
        if stop == name:
            barrier()
            raise _Stop()
    ident = sb("ident", [128, 128], BF16)
    rotm = sb("rotm", [128, 128], BF16)
    maskT = sb("maskT", [128, 2, 128], BF16)
    cols = sb("cols", [128, NCOL], F32)
    ncols = sb("ncols", [128, NCOL], F32)
    gsub = sb("gsub", [128, 2], F32)
    dwk = sb("dwk", [128, KC * CONV_W], F32)
    lamt = sb("lamt", [128, 4, 128], F32)
    lamc = sb("lamc", [128, 8], F32)
    ones_f = sb("ones_f", [128, 128], F32)
    dummy = sb("dummy", [128, 8], F32)
    sm = sb("sm", [128, 64], F32)
    b_const = Buf()
    b_dummy = Buf()

    banks = [nc.alloc_psum_tensor(f"bank{i}", [128, 512], F32) for i in range(8)]
    bank_b = [Buf() for _ in range(8)]
    for _b in bank_b:
        _b.excl = True
    banks_bf = [b.bitcast(BF16) for b in banks]

    sm_ctr = [0]
    sm_bufs = [Buf() for _ in range(64)]

    def sm_alloc():
        i = sm_ctr[0] % 64
        sm_ctr[0] += 1
        return sm[:, i:i + 1], sm_bufs[i]

    sc.op("dve", lambda e: e.memset(dummy[:], 0.0), [], [b_dummy])
    sc.op("act", lambda e: e.activation(out=dummy[:, 0:4], in_=dummy[:, 4:8], func=AF.Copy), [b_dummy], [b_dummy])

    cb = sc.dmabuf()
    cbp = sc.dmabuf()
    for dst, src in ((ident[:], ident_d), (rotm[:], rot_d), (maskT[:], mask_d)):
        sc.dma("pool", dst, src, cbp, [], [b_const])
        sc._wait("pool", [b_const.w])
    sc.dma("sp", cols[:], cols_d, cb, [], [b_const])
    sc._wait("sp", [b_const.w])
    sc.dma("sp", dwk[:], dwk_d, cb, [], [b_const])
    sc._wait("sp", [b_const.w])
    for i in range(4):
        sc.dma("sp", lamt[:, i, :], lam_d[i:i + 1, :].broadcast_to([128, 128]), cb, [], [b_const])
        sc._wait("sp", [b_const.w])
    sc.op("dve", lambda e: e.memset(ones_f[:], 1.0), [], [b_const])
    sc.op("dve", lambda e: e.tensor_scalar(out=ncols[:], in0=cols[:], scalar1=-1.0, scalar2=None, op0=ALU.mult), [b_const], [b_const])
    sc.op("dve", lambda e: e.tensor_scalar(out=gsub[:], in0=cols[:, COL_SUB:COL_SUB + 2], scalar1=1.0 - LAM_INIT, scalar2=None, op0=ALU.mult), [b_const], [b_const])
    sc.op("dve", lambda e: e.tensor_tensor(out=lamt[:, 0, :], in0=lamt[:, 0, :], in1=lamt[:, 1, :], op=ALU.mult), [b_const], [b_const])
    sc.op("dve", lambda e: e.tensor_tensor(out=lamt[:, 2, :], in0=lamt[:, 2, :], in1=lamt[:, 3, :], op=ALU.mult), [b_const], [b_const])
    sc.op("dve", lambda e: e.reduce_sum(out=lamc[:, 0:1], in_=lamt[:, 0, :], axis=AX.X), [b_const], [b_const])
    sc.op("dve", lambda e: e.reduce_sum(out=lamc[:, 1:2], in_=lamt[:, 2, :], axis=AX.X), [b_const], [b_const])
    sc.op("act", lambda e: e.activation(out=lamc[:, 2:4], in_=lamc[:, 0:2], func=AF.Exp), [b_const, b_dummy], [b_const])
    sc.op("dve", lambda e: e.tensor_tensor(out=lamc[:, 4:5], in0=lamc[:, 2:3], in1=lamc[:, 3:4], op=ALU.subtract), [b_const], [b_const])
    sc.op("dve", lambda e: e.tensor_scalar(out=lamc[:, 5:6], in0=lamc[:, 4:5], scalar1=LAM_INIT, scalar2=None, op0=ALU.add), [b_const], [b_const])
    lam_ap = lamc[:, 5:6]
    if stop == "const":
        barrier()
        return nc, stack

    def rstd_chain(ss_ap, ss_buf, scale, eps):
        l_ap, l_buf = sm_alloc()
        r_ap, r_buf = sm_alloc()
        sc.op("act", lambda e: e.activation(out=l_ap, in_=ss_ap, func=AF.Ln, scale=scale, bias=eps), [ss_buf], [l_buf])
        sc.op("act", lambda e: e.activation(out=r_ap, in_=l_ap, func=AF.Exp, scale=-0.5), [l_buf], [r_buf])
        return r_ap, r_buf

    st12 = contextlib.ExitStack()
    scope[0] = st12
    uT = sb("uT", [128, KC, SEQ], BF16)
    uT_b = [Buf() for _ in range(NTB)]
    uT_st = sc.dmabuf()
    st1 = contextlib.ExitStack()
    scope[0] = st1

    gbc = sb("gbc", [128, D], F32)
    gbc_b = sc.dmabuf()
    sc.dma("sp", gbc[:], g_pre_d.broadcast_to([128, D]), gbc_b, [], [gbc_b])
    xt = [sb(f"xt{i}", [128, D], F32) for i in range(2)]
    xt_b = [sc.dmabuf() for _ in range(2)]
    ub = [sb(f"ub{i}", [128, D], BF16) for i in range(2)]
    ub_b = [Buf() for _ in range(2)]
    junk = sb("junk", [128, D], BF16)
    junk_b = Buf()

    def norm_to_T(src_ap, src_buf, g_ap, g_buf, slot, dstT, dst_buf, tcol, eps=1e-6):
        ss_ap, ss_b = sm_alloc()
        sc.op("act", lambda e: e.activation(out=junk[:], in_=src_ap, func=AF.Square, accum_out=ss_ap), [src_buf], [junk_b, ss_b])
        r_ap, r_b = rstd_chain(ss_ap, ss_b, 1.0 / D, eps)
        sc.op("dve", lambda e: e.scalar_tensor_tensor(out=ub[slot][:], in0=src_ap, scalar=r_ap, in1=g_ap, op0=ALU.mult, op1=ALU.mult),
              [src_buf, r_b, g_buf], [ub_b[slot]])
        for half in range(2):
            bk = 4 + 2 * slot + half
            pv = banks_bf[bk].reshape([128, 8, 128])

            def tr(e, half=half, pv=pv):
                ins = None
                for i in range(8):
                    kc = half * 8 + i
                    ins = e.transpose(pv[:, i, :], ub[slot][:, kc * 128:(kc + 1) * 128], ident[:])
                return ins
            sc.op("pe", tr, [ub_b[slot], b_const], [bank_b[bk]])
            eng = "act" if half == 0 else "dve"
            if eng == "act":
                sc.op("act", lambda e, half=half, pv=pv: e.activation(out=dstT[:, half * 8:half * 8 + 8, tcol:tcol + 128], in_=pv[:], func=AF.Copy),
                      [bank_b[bk]], [dst_buf])
            else:
                sc.op("dve", lambda e, half=half, pv=pv: e.tensor_copy(out=dstT[:, half * 8:half * 8 + 8, tcol:tcol + 128], in_=pv[:]),
                      [bank_b[bk]], [dst_buf])

    for tt in range(NT):
        s = tt % 2
        sc.dma("sp", xt[s][:], x_d[tt * 128:(tt + 1) * 128, :], xt_b[s], [], [xt_b[s]])
        norm_to_T(xt[s][:], xt_b[s], gbc[:], gbc_b, s, uT, uT_b[tt // 4], tt * 128)
    sc.dma("sp", uT_v, uT[:], uT_st, uT_b, [uT_dram_b])
    _rem("p1")
    barrier()
    if stop == "p1":
        return nc, stack
    st1.close()
    st2 = contextlib.ExitStack()
    scope[0] = st2

    WSLOT = 2
    wq = [sb(f"wq{i}", [128, KC, 512], BF16) for i in range(WSLOT)]
    wq_b = [sc.dmabuf() for _ in range(WSLOT)]
    wctr = [0]

    def wload(src_ap, shape3=None):
        i = wctr[0] % WSLOT
        wctr[0] += 1
        n = src_ap.shape[-1]
        sc.dma("pool", wq[i][:, :, 0:n], src_ap, wq_b[i], [], [wq_b[i]])
        return wq[i], wq_b[i]

    qT = sb("qT", [128, 2, SEQ], BF16)
    kT = sb("kT", [128, 2, SEQ], BF16)
    qT_b = [Buf() for _ in range(NTB)]
    kT_b = [Buf() for _ in range(NTB)]
    vaug = sb("vaug", [128, NT, 260], BF16)
    v_b = [Buf() for _ in range(NTB)]
    mAh = [sb(f"mAh{i}", [128, 2, SEQ], BF16) for i in range(2)]
    mAh_b = [[Buf() for _ in range(NTB)] for _ in range(2)]
    mAh_st = [sc.dmabuf() for _ in range(2)]
    cs = [sb(f"cs{i}", [128, 2, TB], F32) for i in range(2)]
    cs_b = [sc.dmabuf() for _ in range(2)]
    qraw = [sb(f"qraw{i}", [128, TB], BF16) for i in range(2)]
    qraw_b = [Buf() for _ in range(2)]
    rt1 = [sb(f"rt1_{i}", [128, TB], F32) for i in range(2)]
    rt1_b = [Buf() for _ in range(2)]
    rt2 = [sb(f"rt2_{i}", [128, TB], F32) for i in range(2)]
    rt2_b = [Buf() for _ in range(2)]
    gtmp = [sb(f"gtmp{i}", [128, TB], F32) for i in range(2)]
    gtmp_b = [Buf() for _ in range(2)]
    NE = 4
    et = [sb(f"et{i}", [128, 2, 256], BF16) for i in range(NE)]
    et_b = [Buf() for _ in range(NE)]
    ot = [sb(f"ot{i}", [128, 256], F32) for i in range(2)]
    ot_b = [Buf() for _ in range(2)]
    ot2 = [sb(f"ot2_{i}", [128, 256], F32) for i in range(2)]
    ot2_b = [Buf() for _ in range(2)]
    onb = [sb(f"onb{i}", [128, 256], BF16) for i in range(2)]
    onb_b = [Buf() for _ in range(2)]
    sc.op("dve", lambda e: e.memset(vaug[:, :, 256:260], 1.0), [], [v_b[0]])
    wcc = [sb(f"wcc{i}", [128, KC, 256], BF16) for i in range(2)]
    wcc_b = [sc.dmabuf() for _ in range(2)]
    gtc = [sb(f"gtc{i}", [128, HALO + TB], F32) for i in range(2)]
    gtc_b = [Buf() for _ in range(2)]
    sgc = sb("sgc", [128, TB], F32)
    sgc_b = Buf()
    accc = sb("accc", [128, TB], F32)
    accc_b = Buf()
    yst = [sb(f"yst{i}", [128, TB], F32) for i in range(2)]
    yst_b = [sc.dmabuf() for _ in range(2)]
    tailc = sb("tailc", [128, HALO], F32)
    tailc_b = Buf()
    cu_ctr = [0]
    aev = sb("aev", [128, 2, 2, 260], F32)
    aev_b = Buf()
    pending = []

    def conv_unit(c, tb):
        slot = c % 2
        t0 = tb * TB
        s2 = cu_ctr[0] % 2
        cu_ctr[0] += 1
        if tb == 0:
            sc.dma("pool", wcc[slot][:, :, 0:128], w_in_v[:, :, C_CV + c * 128:C_CV + (c + 1) * 128], wcc_b[slot], [], [wcc_b[slot]])
            sc.dma("pool", wcc[slot][:, :, 128:256], w_in_v[:, :, C_CG + c * 128:C_CG + (c + 1) * 128], wcc_b[slot], [], [wcc_b[slot]])
            sc.op("dve", lambda e: e.memset(tailc[:], 0.0), [], [tailc_b])
        wc, wc_b = wcc[slot], wcc_b[slot]

        def pjg(e):
            ins = None
            for kc in range(KC):
                ins = e.matmul(banks[7][:], lhsT=wc[:, kc, 128:256], rhs=uT[:, kc, t0:t0 + TB], start=(kc == 0), stop=(kc == KC - 1))
            return ins

        def pjv(e):
            ins = None
            for kc in range(KC):
                ins = e.matmul(banks[6][:], lhsT=wc[:, kc, 0:128], rhs=uT[:, kc, t0:t0 + TB], start=(kc == 0), stop=(kc == KC - 1))
            return ins
        sc.op("pe", pjg, [wc_b, uT_b[tb]], [bank_b[7]])
        sc.op("pe", pjv, [wc_b, uT_b[tb]], [bank_b[6]])
        sc.op("act", lambda e: e.activation(out=sgc[:], in_=banks[7][:], func=AF.Exp, scale=-1.0, bias=ncols[:, COL_BG + c:COL_BG + c + 1]),
              [bank_b[7], b_const], [sgc_b])
        sc.op("act", lambda e: e.activation(out=sgc[:], in_=sgc[:], func=AF.Ln, bias=1.0), [sgc_b], [sgc_b])
        sc.op("act", lambda e: e.activation(out=sgc[:], in_=sgc[:], func=AF.Exp, scale=-1.0), [sgc_b], [sgc_b])
        g_ = gtc[s2]
        sc.op("dve", lambda e: e.tensor_copy(out=g_[:, 0:HALO], in_=tailc[:]), [tailc_b], [gtc_b[s2]])
        sc.op("dve", lambda e: e.scalar_tensor_tensor(out=g_[:, HALO:HALO + TB], in0=banks[6][:], scalar=cols[:, COL_BV + c:COL_BV + c + 1],
                                                      in1=sgc[:], op0=ALU.add, op1=ALU.mult), [bank_b[6], sgc_b, b_const], [gtc_b[s2]])
        sc.op("dve", lambda e: e.tensor_copy(out=tailc[:], in_=g_[:, TB:TB + HALO]), [gtc_b[s2]], [tailc_b])

        def wk(j):
            return dwk[:, c * CONV_W + j:c * CONV_W + j + 1]
        bufA = [(accc[:], accc_b), (yst[s2][:], yst_b[s2])]
        bufB = [(rt1[0][:], rt1_b[0]), (rt1[1][:], rt1_b[1])]
        for i in range(16):
            ja = 2 * i
            oa, oa_b = bufA[i % 2]
            if i == 0:
                sc.op("dve", lambda e, oa=oa: e.tensor_scalar(out=oa, in0=g_[:, 0:TB], scalar1=wk(0), scalar2=cols[:, COL_DWB + c:COL_DWB + c + 1],
                                                             op0=ALU.mult, op1=ALU.add), [gtc_b[s2], b_const], [oa_b])
            else:
                ia, ia_b = bufA[(i - 1) % 2]
                sc.op("dve", lambda e, oa=oa, ia=ia, ja=ja: e.scalar_tensor_tensor(out=oa, in0=g_[:, ja:ja + TB], scalar=wk(ja), in1=ia, op0=ALU.mult, op1=ALU.add),
                      [gtc_b[s2], b_const, ia_b], [oa_b])
            if i < 15:
                jb = 2 * i + 1
                ob, ob_b = bufB[i % 2]
                if i == 0:
                    sc.op("dve", lambda e, ob=ob, jb=jb: e.tensor_scalar(out=ob, in0=g_[:, jb:jb + TB], scalar1=wk(jb), scalar2=None, op0=ALU.mult),
                          [gtc_b[s2], b_const], [ob_b])
                else:
                    ib, ib_b = bufB[(i - 1) % 2]
                    sc.op("dve", lambda e, ob=ob, ib=ib, jb=jb: e.scalar_tensor_tensor(out=ob, in0=g_[:, jb:jb + TB], scalar=wk(jb), in1=ib, op0=ALU.mult, op1=ALU.add),
                          [gtc_b[s2], b_const, ib_b], [ob_b])
        yf, yf_b = bufB[1]
        sc.op("dve", lambda e: e.tensor_tensor(out=yf, in0=yst[s2][:], in1=bufB[0][0], op=ALU.add), [yst_b[s2], bufB[0][1]], [yf_b])
        sc.dma("sp", ys_v[:, c, t0:t0 + TB], yf, yst_b[s2], [yf_b], [ys_dram_b[c]])

    pj_ctr = [0]
    ectr = [0]
    fin_ctr = [0]

    for h in range(0 if os.environ.get('KSKIP') else NH):
        sc.rotate()
        hs = h % 2
        wQ, wQ_b = None, None
        wqk, wqk_b = wload(w_in_v[:, :, C_Q + h * 256:C_Q + (h + 1) * 256])
        iqk = (wctr[0] - 1) % WSLOT
        sc.dma("pool", wq[iqk][:, :, 256:512], w_in_v[:, :, C_K + h * 256:C_K + (h + 1) * 256], wq_b[iqk], [], [wq_b[iqk]])
        wvg, wvg_b = wload(w_in_v[:, :, C_V + h * 256:C_V + (h + 1) * 256])
        ivg = (wctr[0] - 1) % WSLOT
        sc.dma("pool", wq[ivg][:, :, 256:512], w_in_v[:, :, C_GA + h * 256:C_GA + (h + 1) * 256], wq_b[ivg], [], [wq_b[ivg]])

        for tb in range(NTB):
            t0 = tb * TB
            cslot = (h * NTB + tb) % 2
            sc.dma("sp", cs[cslot][:, 0, :], cos_d[:, t0:t0 + TB], cs_b[cslot], [], [cs_b[cslot]])
            sc.dma("sp", cs[cslot][:, 1, :], sin_d[:, t0:t0 + TB], cs_b[cslot], [], [cs_b[cslot]])
            units = [(typ, sub) for typ in (0, 1) for sub in (0, 1)]
            pend = []

            def rope_tail(item):
                typ, sub, bk, rs = item
                dstT, dst_b = (qT, qT_b[tb]) if typ == 0 else (kT, kT_b[tb])
                rb = 4 + rs
                sc.op("pe", lambda e: e.matmul(banks[rb][:], lhsT=rotm[:], rhs=qraw[rs][:], start=True, stop=True),
                      [qraw_b[rs], b_const], [bank_b[rb]])
                sc.op("dve", lambda e: e.tensor_tensor(out=rt2[rs][:], in0=banks[rb][:], in1=cs[cslot][:, 1, :], op=ALU.mult),
                      [bank_b[rb], cs_b[cslot]], [rt2_b[rs]])
                sc.op("pool", lambda e: e.tensor_tensor(out=dstT[:, sub, t0:t0 + TB], in0=rt1[rs][:], in1=rt2[rs][:], op=ALU.add),
                      [rt1_b[rs], rt2_b[rs]], [dst_b])

            for (typ, sub) in units:
                bk = pj_ctr[0] % 4
                rs = pj_ctr[0] % 2
                pj_ctr[0] += 1
                col0 = typ * 256 + sub * 128
                scl = (128 ** -0.5) if typ == 0 else 1.0

                def proj(e, col0=col0, bk=bk):
                    ins = None
                    for kc in range(KC):
                        ins = e.matmul(banks[bk][:], lhsT=wqk[:, kc, col0:col0 + 128], rhs=uT[:, kc, t0:t0 + TB],
                                       start=(kc == 0), stop=(kc == KC - 1))
                    return ins
                sc.op("pe", proj, [wqk_b, uT_b[tb]], [bank_b[bk]])
                sc.op("act", lambda e, bk=bk, rs=rs, scl=scl: e.activation(out=qraw[rs][:], in_=banks[bk][:], func=AF.Copy, scale=scl),
                      [bank_b[bk]], [qraw_b[rs]])
                sc.op("dve", lambda e, bk=bk, rs=rs, scl=scl: e.scalar_tensor_tensor(out=rt1[rs][:], in0=banks[bk][:], scalar=scl, in1=cs[cslot][:, 0, :],
                                                                                op0=ALU.mult, op1=ALU.mult),
                      [bank_b[bk], cs_b[cslot]], [rt1_b[rs]])
                pend.append((typ, sub, bk, rs))
                if len(pend) > 1:
                    rope_tail(pend.pop(0))
            for c in range(2):
                bk = pj_ctr[0] % 4
                gs = pj_ctr[0] % 2
                pj_ctr[0] += 1

                def projg(e, c=c, bk=bk):
                    ins = None
                    for kc in range(KC):
                        ins = e.matmul(banks[bk][:], lhsT=wvg[:, kc, 256 + c * 128:256 + (c + 1) * 128], rhs=uT[:, kc, t0:t0 + TB],
                                       start=(kc == 0), stop=(kc == KC - 1))
                    return ins
                sc.op("pe", projg, [wvg_b, uT_b[tb]], [bank_b[bk]])
                if pend:
                    rope_tail(pend.pop(0))
                sc.op("act", lambda e, bk=bk, gs=gs: e.activation(out=gtmp[gs][:], in_=banks[bk][:], func=AF.Exp, scale=-1.0),
                      [bank_b[bk]], [gtmp_b[gs]])
                sc.op("act", lambda e, gs=gs: e.activation(out=gtmp[gs][:], in_=gtmp[gs][:], func=AF.Ln, bias=1.0),
                      [gtmp_b[gs]], [gtmp_b[gs]])
                sc.op("act", lambda e, gs=gs, c=c: e.activation(out=mAh[hs][:, c, t0:t0 + TB], in_=gtmp[gs][:], func=AF.Exp, scale=-1.0),
                      [gtmp_b[gs]], [mAh_b[hs][tb]])
            while pend:
                rope_tail(pend.pop(0))
            for ti in range(4):
                tt = tb * 4 + ti
                bk = pj_ctr[0] % 4
                pj_ctr[0] += 1

                def projv(e, tt=tt, bk=bk):
                    ins = None
                    for kc in range(KC):
                        ins = e.matmul(banks[bk][:, 0:256], lhsT=uT[:, kc, tt * 128:(tt + 1) * 128], rhs=wvg[:, kc, 0:256],
                                       start=(kc == 0), stop=(kc == KC - 1))
                    return ins
                sc.op("pe", projv, [wvg_b, uT_b[tb]], [bank_b[bk]])
                sc.op("act", lambda e, tt=tt, bk=bk: e.activation(out=vaug[:, tt, 0:256], in_=banks[bk][:, 0:256], func=AF.Copy),
                      [bank_b[bk]], [v_b[tb]])

        for g in range(8):
            conv_unit(2 * h + g // 4, g % 4)
            qb0 = 2 * g
            nj = 2 * g + 2
            qtb = (g * 256) // TB
            pendq = []

            def av(j, es):
                for qi in range(2):
                    qb = qb0 + qi
                    if j > qb:
                        continue
                    for sub in range(2):
                        ab = qi * 2 + sub
                        sc.op("pe", lambda e, qi=qi, sub=sub, ab=ab: e.matmul(banks[ab][:, 0:257], lhsT=et[es][:, sub, qi * 128:(qi + 1) * 128],
                                                                              rhs=vaug[:, j, 0:257], start=(j == 0), stop=(j == qb)),
                              [et_b[es], v_b[j // 4]], [bank_b[ab]])

            for j in range(nj):
                sbk = 4 + (ectr[0] % 2)
                es = ectr[0] % NE
                ectr[0] += 1
                sv = banks[sbk].reshape([128, 2, 256])

                def qk(e, j=j, sv=sv):
                    ins = None
                    for sub in range(2):
                        lk = kT[:, sub, j * 128:(j + 1) * 128]
                        if j < qb0:
                            ins = e.matmul(sv[:, sub, :], lhsT=lk, rhs=qT[:, sub, g * 256:(g + 1) * 256], start=True, stop=True)
                            continue
                        qi = j - qb0
                        e.matmul(sv[:, sub, qi * 128:(qi + 1) * 128], lhsT=lk, rhs=qT[:, sub, (qb0 + qi) * 128:(qb0 + qi + 1) * 128], start=True, stop=False)
                        ins = e.matmul(sv[:, sub, qi * 128:(qi + 1) * 128], lhsT=ident[:], rhs=maskT[:, 1, :], start=False, stop=True)
                        if qi == 0:
                            ins = e.matmul(sv[:, sub, 128:256], lhsT=lk, rhs=qT[:, sub, (qb0 + 1) * 128:(qb0 + 2) * 128], start=True, stop=True)
                    return ins
                sc.op("pe", qk, [kT_b[j // 4], qT_b[qtb]], [bank_b[sbk]])
                sc.op("act", lambda e, sv=sv, es=es: e.activation(out=et[es][:], in_=sv[:], func=AF.Exp), [bank_b[sbk]], [et_b[es]])
                pendq.append((j, es))
                if len(pendq) > 1:
                    av(*pendq.pop(0))
            while pendq:
                av(*pendq.pop(0))
            for fn_ in pending:
                fn_()
            pending.clear()
            for qi in range(2):
                for sub in range(2):
                    ab = qi * 2 + sub
                    if True:
                        sc.op("act", lambda e, qi=qi, sub=sub, ab=ab: e.activation(out=aev[:, qi, sub, 0:257], in_=banks[ab][:, 0:257], func=AF.Copy),
                              [bank_b[ab]], [aev_b])
                    else:
                        sc.op("dve", lambda e, qi=qi, sub=sub, ab=ab: e.tensor_copy(out=aev[:, qi, sub, 0:257], in_=banks[ab][:, 0:257]),
                              [bank_b[ab]], [aev_b])

            def finalize(h=h, hs=hs, qb0=qb0):
                for qi in range(2):
                    qb = qb0 + qi
                    fs = fin_ctr[0] % 2
                    fin_ctr[0] += 1
                    a1, a2 = aev[:, qi, 0, :], aev[:, qi, 1, :]
                    r1, r1_b = sm_alloc()
                    r2, r2_b = sm_alloc()
                    sc.op("dve", lambda e: e.reciprocal(out=r1, in_=a1[:, 256:257]), [aev_b], [r1_b])
                    sc.op("dve", lambda e: e.reciprocal(out=r2, in_=a2[:, 256:257]), [aev_b], [r2_b])
                    sc.op("dve", lambda e: e.tensor_tensor(out=r2, in0=r2, in1=lam_ap, op=ALU.mult), [r2_b, b_const], [r2_b])
                    sc.op("dve", lambda e: e.tensor_scalar(out=ot2[fs][:], in0=a2[:, 0:256], scalar1=r2, scalar2=None, op0=ALU.mult),
                          [aev_b, r2_b], [ot2_b[fs]])
                    sc.op("dve", lambda e: e.scalar_tensor_tensor(out=ot[fs][:], in0=a1[:, 0:256], scalar=r1, in1=ot2[fs][:], op0=ALU.mult, op1=ALU.subtract),
                          [aev_b, r1_b, ot2_b[fs]], [ot_b[fs]])
                    ss, ss_b = sm_alloc()
                    sc.op("act", lambda e: e.activation(out=ot2[fs][:], in_=ot[fs][:], func=AF.Square, accum_out=ss), [ot_b[fs], ot2_b[fs]], [ot2_b[fs], ss_b])
                    rs_, rs_b = rstd_chain(ss, ss_b, 1.0 / 256, 1e-5)
                    sc.op("dve", lambda e: e.tensor_scalar(out=onb[fs][:], in0=ot[fs][:], scalar1=rs_, scalar2=None, op0=ALU.mult),
                          [ot_b[fs], rs_b], [onb_b[fs]])
                    pT = banks_bf[7].reshape([128, 8, 128])

                    def tr2(e):
                        ins = None
                        for c in range(2):
                            ins = e.transpose(pT[:, fs * 2 + c, :], onb[fs][:, c * 128:(c + 1) * 128], ident[:])
                        return ins
                    sc.op("pe", tr2, [onb_b[fs], b_const], [bank_b[7]])
                    mtb = (qb * 128) // TB
                    for c in range(2):
                        dst = mAh[hs][:, c, qb * 128:(qb + 1) * 128]
                        sc.op("dve", lambda e, c=c, dst=dst: e.scalar_tensor_tensor(out=dst, in0=pT[:, fs * 2 + c, :], scalar=gsub[:, c:c + 1], in1=dst,
                                                                                  op0=ALU.mult, op1=ALU.mult),
                              [bank_b[7], b_const, mAh_b[hs][mtb]], [mAh_b[hs][mtb]])
            pending.append(finalize)

        def store(h=h, hs=hs):
            sc.dma("sp", mA_v[:, 2 * h:2 * h + 2, :], mAh[hs][:], mAh_st[hs], [mAh_b[hs][i] for i in range(NTB)], [mA_dram_b[h]])
        pending.append(store)
        if stop == "p2h0":
            for fn_ in pending:
                fn_()
            barrier()
            return nc, stack

    for fn_ in pending:
        fn_()
    pending.clear()
    mA_done = mA_dram_b
    _rem("p2")
    barrier()
    if stop == "p2":
        return nc, stack
    st2.close()
    st12.close()
    st3 = contextlib.ExitStack()
    scope[0] = st3
    WSLOT = 3
    wq = [sb(f"wq3_{i}", [128, KC, 512], BF16) for i in range(WSLOT)]
    wq_b = [sc.dmabuf() for _ in range(WSLOT)]
    junk = sb("junk3", [128, D], BF16)
    junk_b = Buf()
    ub = [sb(f"ub3_{i}", [128, D], BF16) for i in range(2)]
    ub_b = [Buf() for _ in range(2)]
    xt = [sb(f"gb3_{i}", [128, D], F32) for i in range(2)]
    xt_b = [sc.dmabuf() for _ in range(2)]
    uTb = sb("uTb", [128, KC, TB], BF16)
    uTb_b = sc.dmabuf()

    gpm = xt[0]
    gpf = xt[1]
    sc.dma("sp", gpm[:], g_pm_d.broadcast_to([128, D]), xt_b[0], [], [xt_b[0]])
    sc.dma("sp", gpf[:], g_pf_d.broadcast_to([128, D]), xt_b[1], [], [xt_b[1]])
    gpm_b, gpf_b = xt_b[0], xt_b[1]
    ybuf = sb("ybuf", [128, KC, TB], F32)
    y_b = [Buf() for _ in range(KC)]
    yld_b = sc.dmabuf()
    cT = sb("cT", [128, KC, TB], BF16)
    cT_b = sc.dmabuf()
    mblk = sb("mblk", [128, KC, TB], BF16)
    mblk_b = sc.dmabuf()
    acc2 = [sb(f"acc2_{i}", [128, TB], F32) for i in range(2)]
    acc2_b = [Buf() for _ in range(2)]
    ysq = [sb(f"ysq{i}", [128, TB], F32) for i in range(2)]
    ysq_b = [Buf() for _ in range(2)]
    sgm = [sb(f"sgm{i}", [128, TB], F32) for i in range(2)]
    sgm_b = [Buf() for _ in range(2)]
    mean_t = sb("mean_t", [128, TB], F32)
    rstd_t = sb("rstd_t", [128, TB], F32)
    stat_b = Buf()
    lt1, lt1_b = acc2, acc2_b
    lt2, lt2_b = ysq, ysq_b
    xr = [sb("xr0", [128, D], F32)] * 2
    xr_b = [sc.dmabuf()] * 2
    h1t = [sb(f"h1t{i}", [128, D], F32) for i in range(2)]
    h1t_b = [sc.dmabuf() for _ in range(2)]
    u2s, u2s_b = cT, cT_b

    pc = [0]
    for tb in range(0 if os.environ.get('KSKIP') else NTB):
        sc.rotate()
        t0 = tb * TB
        sc.dma("sp", mblk[:], mA_v[:, :, t0:t0 + TB], mblk_b, mA_done, [mblk_b])
        sc.dma("sp", uTb[:], uT_v[:, :, t0:t0 + TB], uTb_b, [uT_dram_b], [uTb_b])
        sc.dma("sp", ybuf[:], ys_v[:, :, t0:t0 + TB], yld_b, ys_dram_b, y_b)
        for c in range(KC):
            s2 = c % 2
            yv = ybuf[:, c, :]
            sc.op("act", lambda e, s2=s2, yv=yv: e.activation(out=ysq[s2][:], in_=yv, func=AF.Square), [y_b[c]], [ysq_b[s2]])
            sc.op("pe", lambda e, yv=yv, c=c: e.matmul(banks[4][:], lhsT=ones_f[:], rhs=yv, start=(c == 0), stop=(c == KC - 1)), [y_b[c], b_const], [bank_b[4]])
            sc.op("pe", lambda e, s2=s2, c=c: e.matmul(banks[5][:], lhsT=ones_f[:], rhs=ysq[s2][:], start=(c == 0), stop=(c == KC - 1)), [ysq_b[s2], b_const], [bank_b[5]])
        sc.op("dve", lambda e: e.tensor_scalar(out=mean_t[:], in0=banks[4][:], scalar1=1.0 / D, scalar2=None, op0=ALU.mult), [bank_b[4]], [stat_b])
        sc.op("dve", lambda e: e.tensor_tensor(out=lt1[0][:], in0=mean_t[:], in1=mean_t[:], op=ALU.mult), [stat_b], [lt1_b[0]])
        sc.op("dve", lambda e: e.scalar_tensor_tensor(out=rstd_t[:], in0=banks[5][:], scalar=1.0 / D, in1=lt1[0][:], op0=ALU.mult, op1=ALU.subtract),
              [bank_b[5], lt1_b[0]], [stat_b])
        sc.op("act", lambda e: e.activation(out=rstd_t[:], in_=rstd_t[:], func=AF.Ln, bias=1e-5), [stat_b], [stat_b])
        sc.op("act", lambda e: e.activation(out=rstd_t[:], in_=rstd_t[:], func=AF.Exp, scale=-0.5), [stat_b], [stat_b])
        for c in range(KC):
            s2 = c % 2
            yv = ybuf[:, c, :]
            sc.op("dve", lambda e, s2=s2, yv=yv: e.tensor_tensor(out=lt1[s2][:], in0=yv, in1=mean_t[:], op=ALU.subtract), [y_b[c], stat_b], [lt1_b[s2]])
            sc.op("dve", lambda e, s2=s2: e.tensor_tensor(out=lt2[s2][:], in0=lt1[s2][:], in1=rstd_t[:], op=ALU.mult), [lt1_b[s2], stat_b], [lt2_b[s2]])
            sc.op("act", lambda e, s2=s2, c=c: e.activation(out=lt1[s2][:], in_=lt2[s2][:], func=AF.Exp, scale=ncols[:, COL_LNG + c:COL_LNG + c + 1],
                                                          bias=ncols[:, COL_LNB + c:COL_LNB + c + 1]), [lt2_b[s2], b_const], [lt1_b[s2]])
            sc.op("act", lambda e, s2=s2, c=c: e.activation(out=lt2[s2][:], in_=lt2[s2][:], func=AF.Identity, scale=cols[:, COL_LNG + c:COL_LNG + c + 1],
                                                          bias=cols[:, COL_LNB + c:COL_LNB + c + 1]), [lt2_b[s2], b_const], [lt2_b[s2]])
            sc.op("act", lambda e, s2=s2: e.activation(out=lt1[s2][:], in_=lt1[s2][:], func=AF.Ln, bias=1.0), [lt1_b[s2]], [lt1_b[s2]])
            sc.op("act", lambda e, s2=s2: e.activation(out=lt1[s2][:], in_=lt1[s2][:], func=AF.Exp, scale=-1.0), [lt1_b[s2]], [lt1_b[s2]])
            sc.op("dve", lambda e, s2=s2, c=c: e.tensor_tensor(out=cT[:, c, :], in0=lt2[s2][:], in1=lt1[s2][:], op=ALU.mult), [lt1_b[s2], lt2_b[s2]], [cT_b])
        for fg in range(4):
            wco, wco_b = wload(w_co_v[:, :, fg * 512:(fg + 1) * 512])
            wgc, wgc_b = wload(w_in_v[:, :, C_GC + fg * 512:C_GC + (fg + 1) * 512])
            for fi in range(4):
                f = fg * 4 + fi
                s2 = pc[0] % 2
                pc[0] += 1
                bo, bg = s2 * 2, s2 * 2 + 1

                def pco(e, fi=fi, bo=bo):
                    ins = None
                    for kc in range(KC):
                        ins = e.matmul(banks[bo][:], lhsT=wco[:, kc, fi * 128:(fi + 1) * 128], rhs=cT[:, kc, :], start=(kc == 0), stop=(kc == KC - 1))
                    return ins

                def pgc(e, fi=fi, bg=bg):
                    ins = None
                    for kc in range(KC):
                        ins = e.matmul(banks[bg][:], lhsT=wgc[:, kc, fi * 128:(fi + 1) * 128], rhs=uTb[:, kc, :], start=(kc == 0), stop=(kc == KC - 1))
                    return ins
                sc.op("pe", pgc, [wgc_b, uTb_b], [bank_b[bg]])
                sc.op("pe", pco, [wco_b, cT_b], [bank_b[bo]])
                sc.op("act", lambda e, bg=bg, s2=s2: e.activation(out=sgm[s2][:], in_=banks[bg][:], func=AF.Exp, scale=-1.0), [bank_b[bg]], [sgm_b[s2]])
                sc.op("act", lambda e, s2=s2: e.activation(out=sgm[s2][:], in_=sgm[s2][:], func=AF.Ln, bias=1.0), [sgm_b[s2]], [sgm_b[s2]])
                sc.op("act", lambda e, s2=s2: e.activation(out=sgm[s2][:], in_=sgm[s2][:], func=AF.Exp, scale=-1.0), [sgm_b[s2]], [sgm_b[s2]])
                sc.op("dve", lambda e, s2=s2, bo=bo, f=f: e.scalar_tensor_tensor(out=acc2[s2][:], in0=banks[bo][:], scalar=cols[:, COL_BCO + f:COL_BCO + f + 1], in1=sgm[s2][:],
                                                                               op0=ALU.add, op1=ALU.mult), [bank_b[bo], sgm_b[s2], b_const], [acc2_b[s2]])
                sc.op("dve", lambda e, s2=s2, f=f: e.tensor_tensor(out=mblk[:, f, :], in0=acc2[s2][:], in1=mblk[:, f, :], op=ALU.add), [acc2_b[s2], mblk_b], [mblk_b])
        o2 = ybuf.reshape([128, 4, D])
        for nb in range(4):
            wo, wo_b = wload(w_out_v[:, :, nb * 512:(nb + 1) * 512])
            for ti in range(4):
                bk = ti

                def pwo(e, ti=ti, bk=bk):
                    ins = None
                    for kc in range(KC):
                        ins = e.matmul(banks[bk][:], lhsT=mblk[:, kc, ti * 128:(ti + 1) * 128], rhs=wo[:, kc, :], start=(kc == 0), stop=(kc == KC - 1))
                    return ins
                sc.op("pe", pwo, [wo_b, mblk_b, cT_b], [bank_b[bk]])
                if ti % 2 == 0:
                    sc.op("act", lambda e, ti=ti, bk=bk, nb=nb: e.activation(out=o2[:, ti, nb * 512:(nb + 1) * 512], in_=banks[bk][:], func=AF.Copy), [bank_b[bk]], y_b)
                else:
                    sc.op("dve", lambda e, ti=ti, bk=bk, nb=nb: e.tensor_copy(out=o2[:, ti, nb * 512:(nb + 1) * 512], in_=banks[bk][:]), [bank_b[bk]], y_b)
        for ti in range(4):
            tt = tb * 4 + ti
            s2 = tt % 2
            sc.dma("sp", xr[s2][:], x_d[tt * 128:(tt + 1) * 128, :], xr_b[s2], [], [xr_b[s2]])
            ss, ss_b = sm_alloc()
            sc.op("act", lambda e, ti=ti: e.activation(out=junk[:], in_=o2[:, ti, :], func=AF.Square, accum_out=ss), y_b, [junk_b, ss_b])
            r_ap, r_b = rstd_chain(ss, ss_b, 1.0 / D, 1e-6)
            sc.op("dve", lambda e, ti=ti, s2=s2: e.scalar_tensor_tensor(out=h1t[s2][:], in0=o2[:, ti, :], scalar=r_ap, in1=gpm[:], op0=ALU.mult, op1=ALU.mult),
                  y_b + [r_b, gpm_b], [h1t_b[s2]])
            sc.op("pool", lambda e, s2=s2: e.tensor_tensor(out=h1t[s2][:], in0=h1t[s2][:], in1=xr[s2][:], op=ALU.add), [h1t_b[s2], xr_b[s2]], [h1t_b[s2]])
            sc.dma("sp", h1_d[tt * 128:(tt + 1) * 128, :], h1t[s2][:], h1t_b[s2], [h1t_b[s2]], [h1_dram_b[tt]])
            norm_to_T(h1t[s2][:], h1t_b[s2], gpf[:], gpf_b, s2, u2s, u2s_b, ti * 128)
        sc.dma("sp", u2T_v[:, :, t0:t0 + TB], u2s[:], u2s_b, [u2s_b], [u2T_dram_b[tb]])

    _rem("p3")
    barrier()
    if stop == "p3":
        return nc, stack
    st3.close()
    st4 = contextlib.ExitStack()
    scope[0] = st4
    wq = [sb(f"wq4_{i}", [128, KC, 512], BF16) for i in range(WSLOT)]
    wq_b = [sc.dmabuf() for _ in range(WSLOT)]
    junk = sb("junk4", [128, D], BF16)
    junk_b = Buf()
    xt = [sb("gb4", [128, D], F32)]
    xt_b = [sc.dmabuf()]
    xr = [sb("xr4_0", [128, D], F32)] * 2
    xr_b = [sc.dmabuf()] * 2
    h1t = [sb(f"h1t4_{i}", [128, D], F32) for i in range(2)]
    h1t_b = [sc.dmabuf() for _ in range(2)]
    fo_t = sb("fo_t", [128, 4, D], F32)
    y_b = [Buf()]
    hT = sb("hT", [128, 64, TB], BF16)
    hT_b = [Buf() for _ in range(16)]
    u2b = sb("u2b", [128, KC, TB], BF16)
    u2b_b = sc.dmabuf()
    rl = [sb(f"rl{i}", [128, TB], BF16) for i in range(2)]
    rl_b = [Buf() for _ in range(2)]
    fo = fo_t
    gpo = xt[0]
    sc.dma("sp", gpo[:], g_po_d.broadcast_to([128, D]), xt_b[0], [], [xt_b[0]])
    gpo_b = xt_b[0]
    outs = []
    fc = [0]
    for tb in range(NTB):
        sc.rotate()
        t0 = tb * TB
        sc.dma("sp", u2b[:], u2T_v[:, :, t0:t0 + TB], u2b_b, [u2T_dram_b[tb]], [u2b_b])
        for jb in range(16):
            w1, w1_b = wload(w_ff1_v[:, :, jb * 512:(jb + 1) * 512])
            for jc in range(4):
                j = jb * 4 + jc
                s2 = fc[0] % 2
                fc[0] += 1
                bk = 4 + s2

                def pf1(e, jc=jc, bk=bk):
                    ins = None
                    for kc in range(KC):
                        ins = e.matmul(banks[bk][:], lhsT=w1[:, kc, jc * 128:(jc + 1) * 128], rhs=u2b[:, kc, :], start=(kc == 0), stop=(kc == KC - 1))
                    return ins
                sc.op("pe", pf1, [w1_b, u2b_b], [bank_b[bk]])
                sc.op("act", lambda e, bk=bk, s2=s2: e.activation(out=rl[s2][:], in_=banks[bk][:], func=AF.Relu), [bank_b[bk]], [rl_b[s2]])
                sc.op("dve", lambda e, s2=s2, j=j: e.tensor_tensor(out=hT[:, j, :], in0=rl[s2][:], in1=rl[s2][:], op=ALU.mult), [rl_b[s2]], [hT_b[jb]])
        if stop == "p4a":
            barrier()
            return nc, stack
        for nb in range(4):
            for jg in range(4):
                w2, w2_b = wload(w_ff2_v[:, jg, :, nb * 512:(nb + 1) * 512])
                for ti in range(4):
                    def pf2(e, ti=ti, jg=jg):
                        ins = None
                        for kc in range(KC):
                            ins = e.matmul(banks[ti][:], lhsT=hT[:, jg * 16 + kc, ti * 128:(ti + 1) * 128], rhs=w2[:, kc, :],
                                           start=(jg == 0 and kc == 0), stop=(jg == 3 and kc == KC - 1))
                        return ins
                    sc.op("pe", pf2, [w2_b] + hT_b[jg * 4:(jg + 1) * 4], [bank_b[ti]])
            for ti in range(4):
                if ti % 2 == 0:
                    sc.op("act", lambda e, ti=ti, nb=nb: e.activation(out=fo[:, ti, nb * 512:(nb + 1) * 512], in_=banks[ti][:], func=AF.Copy), [bank_b[ti]], y_b)
                else:
                    sc.op("dve", lambda e, ti=ti, nb=nb: e.tensor_copy(out=fo[:, ti, nb * 512:(nb + 1) * 512], in_=banks[ti][:]), [bank_b[ti]], y_b)
        if stop == "p4b":
            barrier()
            return nc, stack
        for ti in range(4):
            tt = tb * 4 + ti
            s2 = tt % 2
            sc.dma("sp", xr[s2][:], h1_d[tt * 128:(tt + 1) * 128, :], xr_b[s2], [h1_dram_b[tt]], [xr_b[s2]])
            ss, ss_b = sm_alloc()
            sc.op("act", lambda e, ti=ti: e.activation(out=junk[:], in_=fo[:, ti, :], func=AF.Square, accum_out=ss), y_b, [junk_b, ss_b])
            r_ap, r_b = rstd_chain(ss, ss_b, 1.0 / D, 1e-6)
            sc.op("dve", lambda e, ti=ti, s2=s2: e.scalar_tensor_tensor(out=h1t[s2][:], in0=fo[:, ti, :], scalar=r_ap, in1=gpo[:], op0=ALU.mult, op1=ALU.mult),
                  y_b + [r_b, gpo_b], [h1t_b[s2]])
            sc.op("pool", lambda e, s2=s2: e.tensor_tensor(out=h1t[s2][:], in0=h1t[s2][:], in1=xr[s2][:], op=ALU.add), [h1t_b[s2], xr_b[s2]], [h1t_b[s2]])
            outs.append(sc.dma("sp", out_d[tt * 128:(tt + 1) * 128, :], h1t[s2][:], h1t_b[s2], [h1t_b[s2]], []))
        if stop == "p4c":
            barrier()
            return nc, stack
    sc._wait("sp", outs)
    _rem("p4")
    barrier()
    st4.close()
    return nc, stack


def _host_consts():
    inv_freq = 1.0 / (10000.0 ** (np.arange(0, 128, 2, dtype=np.float32) / 128.0))
    pos = np.arange(SEQ, dtype=np.float32)
    ang = pos[None, :] * inv_freq[:, None]
    cos = np.cos(ang).astype(np.float32)
    sin = np.sin(ang).astype(np.float32)
    cosT = np.concatenate([cos, cos], 0)
    sinT = np.concatenate([-sin, sin], 0)
    ident = np.eye(128, dtype=np.float32)
    rot = np.zeros((128, 128), np.float32)
    for m in range(128):
        rot[(m + 64) % 128, m] = 1.0
    kq = (np.arange(128)[None, :] >= np.arange(128)[:, None]).astype(np.float32)
    mask = np.stack([kq, (kq - 1.0) * 30000.0], 1)
    return dict(cosT=np.ascontiguousarray(cosT), sinT=np.ascontiguousarray(sinT), ident=ident, rotm=rot, maskT=np.ascontiguousarray(mask))


def _colmajor(v, n):
    return np.asarray(v, np.float32).reshape(n, 128).T


_CACHE = {}


def kernel(x, pre_mix_gain, w_in, lambda_q1, lambda_k1, lambda_q2, lambda_k2, subln_gain, glu_bias, dw_kernel, dw_bias,
           conv_ln_gain, conv_ln_bias, w_conv_out, b_conv_out, w_out, post_mix_gain, pre_ff_gain, w_ff1, w_ff2, post_ff_gain,
           _debug=False, _cores=8, _stop=None, _trace=False):
    f = lambda a: np.ascontiguousarray(np.asarray(a, dtype=np.float32))
    x = f(x)
    cols = np.concatenate([
        _colmajor(f(glu_bias)[0, :2048], 16), _colmajor(f(glu_bias)[0, 2048:], 16), _colmajor(f(dw_bias)[0], 16),
        _colmajor(f(conv_ln_gain)[0], 16), _colmajor(f(conv_ln_bias)[0], 16), _colmajor(f(b_conv_out)[0], 16),
        _colmajor(f(subln_gain)[0], 2)], axis=1)
    dwk = f(dw_kernel)[0].reshape(CONV_W, 16, 128).transpose(2, 1, 0).reshape(128, 16 * CONV_W)
    lamv = np.concatenate([f(lambda_q1), f(lambda_k1), f(lambda_q2), f(lambda_k2)], 0)
    shared = dict(
        w_in=f(w_in)[0], w_conv_out=f(w_conv_out)[0], w_out=f(w_out)[0], w_ff1=f(w_ff1)[0], w_ff2=f(w_ff2)[0],
        pre_mix_gain=f(pre_mix_gain), post_mix_gain=f(post_mix_gain), pre_ff_gain=f(pre_ff_gain), post_ff_gain=f(post_ff_gain),
        cols=np.ascontiguousarray(cols), dwk=np.ascontiguousarray(dwk), lamv=np.ascontiguousarray(lamv), **_host_consts())
    nc, stack = build_nc(debug=_debug, stop=_stop)
    in_maps = [dict(shared, x=x[b]) for b in range(_cores)]
    res = run_bass_kernel_spmd(nc, in_maps, core_ids=list(range(_cores)), **({'trace': True} if _trace else {}))
    if _trace:
        print('EXEC_TIME_NS', res.exec_time_ns)
    if _debug:
        return res
    return np.stack([np.asarray(r["out"], dtype=np.float32) for r in res.results], 0)
```
